# Optimizing a Trainium2 kernel written in Bass

```python
import jax, jax.numpy as jnp
from jax import lax
import numpy as np

D_MODEL = 1024
BATCH = 4
SEQ = 4096
DEPTH = 4
DEC_BATCH = 8
DEC_SEQ = 32
PAST_LEN = 1024

CHUNK = 64
N_MIXERS = 2
N_SGU_LAYERS = (DEPTH + 1) // 2
N_CONV_LAYERS = DEPTH // 2
SGU_CHUNK = 128
SGU_HEADS = 4
D_SGU_FFN = 6 * D_MODEL
D_SGU = D_SGU_FFN // 2
SGU_HEAD_DIM = D_SGU // SGU_HEADS
CONV_WIDTH = 31
CONV_STATE = CONV_WIDTH - 1
D_FF = -(-8 * D_MODEL // (3 * 256)) * 256
RMS_EPS = 1e-6
LN_EPS = 1e-5

kernel_name = "streaming_gmlp_conformer_conv_hybrid"


def rms_norm(x, g):
    xf = x.astype(jnp.float32)
    y = xf * lax.rsqrt(jnp.mean(xf * xf, axis=-1, keepdims=True) + RMS_EPS)
    return (y * g.astype(jnp.float32)).astype(x.dtype)


def layer_norm(x, g, b):
    xf = x.astype(jnp.float32)
    mu = jnp.mean(xf, axis=-1, keepdims=True)
    xc = xf - mu
    var = jnp.mean(xc * xc, axis=-1, keepdims=True)
    y = xc * lax.rsqrt(var + LN_EPS)
    return (y * g.astype(jnp.float32) + b.astype(jnp.float32)).astype(x.dtype)


def sgu_mixer(h, w_in, b_in, ln_g, ln_b, w_s, b_s, w_out, b_out):
    B, T, _ = h.shape
    z = jax.nn.gelu(h @ w_in + b_in)
    u, v = z[..., :D_SGU], z[..., D_SGU:]
    v = layer_norm(v, ln_g, ln_b)
    L = min(T, SGU_CHUNK)
    C = T // L
    mask = jnp.tril(jnp.ones((SGU_CHUNK, SGU_CHUNK), dtype=w_s.dtype))
    ws = (w_s * mask)[:, :L, :L]
    vc = v.reshape(B, C, L, SGU_HEADS, SGU_HEAD_DIM)
    mixed = jnp.einsum('hij,bcjhd->bcihd', ws, vc) + b_s[:, :L].T[None, None, :, :, None]
    gated = u * mixed.reshape(B, T, D_SGU)
    y = gated @ w_out + b_out
    return y, v


def conv_mixer(h, past, w_pw1, b_pw1, w_dw, b_dw, ln_g, ln_b, w_pw2, b_pw2):
    a = h @ w_pw1 + b_pw1
    glu = a[..., :D_MODEL] * jax.nn.sigmoid(a[..., D_MODEL:])
    padded = jnp.concatenate([past, glu], axis=1)
    new_state = padded[:, -CONV_STATE:]
    c = lax.conv_general_dilated(
        padded, w_dw[:, None, :], window_strides=(1,), padding='VALID',
        dimension_numbers=('NWC', 'WIO', 'NWC'), feature_group_count=D_MODEL) + b_dw
    c = jax.nn.silu(layer_norm(c, ln_g, ln_b))
    y = c @ w_pw2 + b_pw2
    return y, new_state


def swiglu(h, w_gate, w_up, w_down):
    return (jax.nn.silu(h @ w_gate) * (h @ w_up)) @ w_down


def trunk(x, conv_past, norm_mix_g, norm_ffn_g, norm_final_g,
          sgu_w_in, sgu_b_in, sgu_ln_g, sgu_ln_b, sgu_w_s, sgu_b_s, sgu_w_out, sgu_b_out,
          conv_w_pw1, conv_b_pw1, conv_w_dw, conv_b_dw, conv_ln_g, conv_ln_b, conv_w_pw2, conv_b_pw2,
          ffn_w_gate, ffn_w_up, ffn_w_down):
    conv_states, sgu_vs = [], []
    for i in range(DEPTH):
        h = rms_norm(x, norm_mix_g[i])
        j = i // N_MIXERS
        if i % N_MIXERS == 0:
            y, v = sgu_mixer(h, sgu_w_in[j], sgu_b_in[j], sgu_ln_g[j], sgu_ln_b[j],
                             sgu_w_s[j], sgu_b_s[j], sgu_w_out[j], sgu_b_out[j])
            sgu_vs.append(v)
        else:
            y, st = conv_mixer(h, conv_past[j], conv_w_pw1[j], conv_b_pw1[j], conv_w_dw[j],
                               conv_b_dw[j], conv_ln_g[j], conv_ln_b[j], conv_w_pw2[j], conv_b_pw2[j])
            conv_states.append(st)
        x = x + y
        h = rms_norm(x, norm_ffn_g[i])
        x = x + swiglu(h, ffn_w_gate[i], ffn_w_up[i], ffn_w_down[i])
    return rms_norm(x, norm_final_g), jnp.stack(conv_states), jnp.stack(sgu_vs)


def setup_inputs(seed: int = 0) -> dict:
    key = jax.random.key(seed)
    ks = jax.random.split(key, 32)
    f32 = jnp.float32

    def nrm(k, shape, scale):
        return jax.random.normal(k, shape, f32) * scale

    def gain(k, shape):
        return 1.0 + 0.02 * jax.random.normal(k, shape, f32)

    return {
        "x_prompt": nrm(ks[0], (BATCH, SEQ, D_MODEL), 1.0),
        "x_sample": nrm(ks[1], (DEC_BATCH, DEC_SEQ, D_MODEL), 1.0),
        "state_conv": nrm(ks[2], (N_CONV_LAYERS, DEC_BATCH, CONV_STATE, D_MODEL), 0.5),
        "norm_mix_g": gain(ks[3], (DEPTH, D_MODEL)),
        "norm_ffn_g": gain(ks[4], (DEPTH, D_MODEL)),
        "norm_final_g": gain(ks[5], (D_MODEL,)),
        "sgu_w_in": nrm(ks[6], (N_SGU_LAYERS, D_MODEL, D_SGU_FFN), D_MODEL ** -0.5),
        "sgu_b_in": nrm(ks[7], (N_SGU_LAYERS, D_SGU_FFN), 0.02),
        "sgu_ln_g": gain(ks[8], (N_SGU_LAYERS, D_SGU)),
        "sgu_ln_b": nrm(ks[9], (N_SGU_LAYERS, D_SGU), 0.02),
        "sgu_w_s": nrm(ks[10], (N_SGU_LAYERS, SGU_HEADS, SGU_CHUNK, SGU_CHUNK), 0.5 * SGU_CHUNK ** -0.5),
        "sgu_b_s": gain(ks[11], (N_SGU_LAYERS, SGU_HEADS, SGU_CHUNK)),
        "sgu_w_out": nrm(ks[12], (N_SGU_LAYERS, D_SGU, D_MODEL), D_SGU ** -0.5),
        "sgu_b_out": nrm(ks[13], (N_SGU_LAYERS, D_MODEL), 0.02),
        "conv_w_pw1": nrm(ks[14], (N_CONV_LAYERS, D_MODEL, 2 * D_MODEL), D_MODEL ** -0.5),
        "conv_b_pw1": nrm(ks[15], (N_CONV_LAYERS, 2 * D_MODEL), 0.02),
        "conv_w_dw": nrm(ks[16], (N_CONV_LAYERS, CONV_WIDTH, D_MODEL), CONV_WIDTH ** -0.5),
        "conv_b_dw": nrm(ks[17], (N_CONV_LAYERS, D_MODEL), 0.02),
        "conv_ln_g": gain(ks[18], (N_CONV_LAYERS, D_MODEL)),
        "conv_ln_b": nrm(ks[19], (N_CONV_LAYERS, D_MODEL), 0.02),
        "conv_w_pw2": nrm(ks[20], (N_CONV_LAYERS, D_MODEL, D_MODEL), D_MODEL ** -0.5),
        "conv_b_pw2": nrm(ks[21], (N_CONV_LAYERS, D_MODEL), 0.02),
        "ffn_w_gate": nrm(ks[22], (DEPTH, D_MODEL, D_FF), D_MODEL ** -0.5),
        "ffn_w_up": nrm(ks[23], (DEPTH, D_MODEL, D_FF), D_MODEL ** -0.5),
        "ffn_w_down": nrm(ks[24], (DEPTH, D_FF, D_MODEL), D_FF ** -0.5),
    }


def reference(x_prompt, x_sample, state_conv, norm_mix_g, norm_ffn_g, norm_final_g,
              sgu_w_in, sgu_b_in, sgu_ln_g, sgu_ln_b, sgu_w_s, sgu_b_s, sgu_w_out, sgu_b_out,
              conv_w_pw1, conv_b_pw1, conv_w_dw, conv_b_dw, conv_ln_g, conv_ln_b, conv_w_pw2, conv_b_pw2,
              ffn_w_gate, ffn_w_up, ffn_w_down):
    weights = (norm_mix_g, norm_ffn_g, norm_final_g,
               sgu_w_in, sgu_b_in, sgu_ln_g, sgu_ln_b, sgu_w_s, sgu_b_s, sgu_w_out, sgu_b_out,
               conv_w_pw1, conv_b_pw1, conv_w_dw, conv_b_dw, conv_ln_g, conv_ln_b, conv_w_pw2, conv_b_pw2,
               ffn_w_gate, ffn_w_up, ffn_w_down)
    zero_past = jnp.zeros((N_CONV_LAYERS, x_prompt.shape[0], CONV_STATE, D_MODEL), dtype=x_prompt.dtype)
    y_prompt, new_conv_prompt, _ = trunk(x_prompt, zero_past, *weights)
    y_sample, new_conv_sample, new_sgu_v_sample = trunk(x_sample, state_conv, *weights)
    return (y_prompt, y_sample, new_conv_prompt, new_conv_sample, new_sgu_v_sample)
```

```python
import sys
import numpy as np
from contextlib import ExitStack
import concourse.bass as bass
import concourse.mybir as mybir
from concourse.bass_utils import run_bass_kernel_spmd

F32 = mybir.dt.float32
BF16 = mybir.dt.bfloat16
AF = mybir.ActivationFunctionType
ALU = mybir.AluOpType

NCORES = 8
D = 1024
KC = 8
DEPTH = 4
DSGU = 3072
DFF = 2816
T_ALL = 256 + 2048 + 32
ST_COLS = [(0, 1152), (1152, 1184)]
RMS_EPS = 1e-6
LN_EPS = 1e-5
SLOT = 4096
NSLOT = 3
NDVE = 7

PP_LAYOUT = {}
_off = 0
for _name, _n in [("mixg", 32), ("ffng", 32), ("fing", 8), ("binu", 48), ("lng", 48), ("lnb", 48),
                  ("bout", 16), ("bpw1", 32), ("wdw", 496), ("bdw", 16), ("clng", 16), ("clnb", 16),
                  ("bpw2", 16)]:
    PP_LAYOUT[_name] = _off
    _off += _n
PP_COLS = _off


class Prog:
    ENGS = ("pe", "act", "dve", "pool", "sp")

    def __init__(self, nc, es):
        self.nc = nc
        self.es = es
        self.ops = {e: [] for e in self.ENGS}
        self.sem = {e: es.enter_context(nc.semaphore("s_" + e)) for e in self.ENGS}
        self.cnt = {e: 0 for e in self.ENGS}
        self.nins = {e: 0 for e in self.ENGS}
        self.last_ms_ins = {e: -10 for e in self.ENGS}
        self.waited = {e: {} for e in self.ENGS}
        self.res = {}
        self.dma_sems = {}
        self.phase = ""
        self.pe_log = []

    def _r(self, key):
        r = self.res.get(key)
        if r is None:
            r = [None, []]
            self.res[key] = r
        return r

    def _collect(self, reads, writes):
        deps = {}

        def add(d, true_dep):
            if d is None:
                return
            k, h, v = d
            if k not in deps or deps[k][1] < v:
                deps[k] = (h, v, true_dep or (k in deps and deps[k][2]))
            elif true_dep and deps[k][1] == v:
                deps[k] = (h, v, True)
        for r in reads:
            add(self._r(r)[0], True)
        for w in writes:
            rr = self._r(w)
            add(rr[0], True)
            for d in rr[1]:
                add(d, False)
        return deps

    def _emit_waits(self, eng, deps):
        for k, (h, v, true_dep) in deps.items():
            if k == eng:
                if eng in ("pe", "sp", "pool") or not true_dep:
                    continue
            if self.waited[eng].get(k, -1) >= v:
                continue
            self.waited[eng][k] = v
            self.ops[eng].append(lambda e, h=h, v=v: e.wait_ge(h, v))

    def _record(self, d, reads, writes):
        for r in reads:
            lst = self._r(r)[1]
            lst[:] = [x for x in lst if x[0] != d[0]]
            lst.append(d)
        for w in writes:
            rr = self._r(w)
            rr[0] = d
            rr[1] = []

    def op(self, eng, fns, reads=(), writes=()):
        if callable(fns):
            fns = [fns]
        deps = self._collect(reads, writes)
        self._emit_waits(eng, deps)
        if eng == "pe":
            self.pe_log.append((sys._getframe(1).f_lineno, len(fns)))
        self.cnt[eng] += 1
        v = self.cnt[eng]
        h = self.sem[eng]
        for f in fns[:-1]:
            self.ops[eng].append(f)
        last = fns[-1]
        self.ops[eng].append(lambda e, last=last, h=h: last(e).then_inc(h, 1))
        self.nins[eng] += len(fns)
        self.last_ms_ins[eng] = self.nins[eng] - 1
        d = (eng, h, v)
        self._record(d, reads, writes)
        return d

    def dma(self, q, fn, semname, reads=(), writes=(), skip_own=False):
        if semname not in self.dma_sems:
            self.dma_sems[semname] = [self.es.enter_context(self.nc.semaphore("d_" + semname)), 0]
        deps = self._collect(reads, writes)
        if skip_own:
            deps.pop("d_" + semname, None)
        self._emit_waits(q, deps)
        ent = self.dma_sems[semname]
        ent[1] += 16
        h, v = ent[0], ent[1]
        self.ops[q].append(lambda e, h=h: fn(e).then_inc(h, 16))
        self.nins[q] += 1
        d = ("d_" + semname, h, v)
        self._record(d, reads, writes)
        return d

    def barrier(self, engs=("pe", "act", "dve")):
        for e1 in engs:
            for e2 in engs:
                if e1 == e2 or self.cnt[e2] == 0:
                    continue
                v = self.cnt[e2]
                if self.waited[e1].get(e2, -1) >= v:
                    continue
                self.waited[e1][e2] = v
                self.ops[e1].append(lambda e, h=self.sem[e2], v=v: e.wait_ge(h, v))

    def finish(self, eng="sp"):
        for name, (h, v) in self.dma_sems.items():
            self.ops[eng].append(lambda e, h=h, v=v: e.wait_ge(h, v))
        for e2 in self.ENGS:
            if e2 != eng and self.cnt[e2] > 0:
                self.ops[eng].append(lambda e, h=self.sem[e2], v=self.cnt[e2]: e.wait_ge(h, v))

    def run_block(self):
        with self.nc.Block() as block:
            @block.tensor
            def _(e):
                for f in self.ops["pe"]:
                    f(e)

            @block.scalar
            def _(e):
                for f in self.ops["act"]:
                    f(e)

            @block.vector
            def _(e):
                for f in self.ops["dve"]:
                    f(e)

            @block.gpsimd
            def _(e):
                for f in self.ops["pool"]:
                    f(e)

            @block.sync
            def _(e):
                for f in self.ops["sp"]:
                    f(e)


def build_program(nlayers=DEPTH, nst=2):
    nc = bass.Bass("TRN2", target_bir_lowering=False)

    def din(name, shape):
        return nc.dram_tensor(name, list(shape), F32, kind="ExternalInput").ap()

    def dout(name, shape):
        return nc.dram_tensor(name, list(shape), F32, kind="ExternalOutput").ap()

    xT = din("xT", [128, KC, T_ALL])
    flag_d = din("flag", [128, 1])
    pastT = din("pastT", [128, 2, KC, 30])
    pp_d = din("pp", [128, PP_COLS])
    mask_d = din("mask", [128, 128])
    ident_d = din("ident", [128, 128])
    wsT_d = din("wsT", [2, 128, 4, 128])
    bs_d = din("bs", [2, 512])
    binv_d = din("binv", [2, DSGU])
    lngr_d = din("lng_row", [2, DSGU])
    lnbr_d = din("lnb_row", [2, DSGU])
    w_in = din("sgu_w_in", [2, D, 2 * DSGU])
    w_out = din("sgu_w_out", [2, DSGU, D])
    w_pw1 = din("conv_w_pw1", [2, D, 2 * D])
    w_pw2 = din("conv_w_pw2", [2, D, D])
    w_gate = din("ffn_w_gate", [DEPTH, D, DFF])
    w_up = din("ffn_w_up", [DEPTH, D, DFF])
    w_down = din("ffn_w_down", [DEPTH, DFF, D])

    yT = dout("yT", [128, KC, 2080])
    convT = dout("convT", [128, 2, 2, KC, 30])
    vout = dout("vout", [2, 32, DSGU])

    with ExitStack() as es:
        P = Prog(nc, es)

        def sb(name, shape, dt):
            return es.enter_context(nc.sbuf_tensor(name, list(shape), dt))

        xA = sb("xA", [128, KC, 1184], F32)
        hA = sb("hA", [128, KC, 1184], BF16)
        BIG = sb("BIG", [128, 30720], BF16)
        AUX = sb("AUX", [128, 13248], BF16)
        RING = sb("RING", [128, NSLOT, SLOT], BF16)
        SQ = sb("SQ", [128, 2, KC, 416], BF16)
        FT = sb("FT", [128, 6, 416], F32)
        WSF = sb("WSF", [128, 4, 128], F32)
        WSM = sb("WSM", [128, 4, 128], BF16)
        BSB = sb("BSB", [128, 4, 128], F32)
        MASK = sb("MASK", [128, 128], F32)
        IDENTB = sb("IDENTB", [128, 128], BF16)
        ONES = sb("ONES", [128, 128], BF16)
        PPS = sb("PPS", [128, PP_COLS], F32)
        PASTF = sb("PASTF", [128, 2, KC, 30], F32)
        TAIL = sb("TAIL", [128, 2, KC, 30], BF16)
        STATS = sb("STATS", [128, 10, 6, 6], F32)
        MV = sb("MV", [128, 10, 2], F32)
        SC = sb("SC", [128, 4, 10], F32)
        EPS = sb("EPS", [128, 2], F32)
        FLAG = sb("FLAG", [128, 1], F32)
        ps = es.enter_context(nc.psum_tensor("ps", [128, 8, 512], F32))
        VS = SQ[:].rearrange("p a k n -> p (a k n)")[:, 0:2 * DSGU].bitcast(F32)

        VN = BIG[:, 0:10 * DSGU].rearrange("p (c n) -> p c n", n=DSGU)
        A_ = BIG[:, 0:22 * 1184].rearrange("p (c n) -> p c n", n=1184)
        PD = BIG[:, 0:KC * 1246].rearrange("p (c n) -> p c n", n=1246)
        CC = BIG[:, 9984:9984 + 2 * KC * 1184].bitcast(F32).rearrange("p (c n) -> p c n", n=1184)
        YS = BIG[:, 0:2 * KC * 416].bitcast(F32).rearrange("p (c n) -> p c n", n=416)
        UG = AUX[:, 0:6 * 1184].rearrange("p (c n) -> p c n", n=1184)
        CB = AUX[:, 7104:7104 + 2 * 24 * 128].bitcast(F32).rearrange("p (c n) -> p c n", n=128)
        _c0 = 7104
        GB = FT[:, 0:3, :].rearrange("p a n -> p (a n)")[:, 0:1024].rearrange("p (c n) -> p c n", n=512)
        BRF = AUX[:, _c0 + 2048:_c0 + 3072].bitcast(F32)
        BRL32 = AUX[:, _c0 + 3072:_c0 + 4096].bitcast(F32)
        BRH2 = AUX[:, _c0 + 4096:_c0 + 5120].rearrange("p (c n) -> p c n", n=512)
        BRLO = AUX[:, _c0 + 5120:_c0 + 5632]
        DIAG = AUX[:, 0:2 * 31 * 128].rearrange("p (b k n) -> p b k n", b=2, k=31)
        CST = AUX[:, 7936:7936 + 2 * KC * 62].bitcast(F32).rearrange("p (c n) -> p c n", n=62)

        def pcol(name, idx):
            o = PP_LAYOUT[name] + idx
            return PPS[:, o:o + 1]

        bank_ctr = [0]

        def newbank():
            b = bank_ctr[0] % 8
            bank_ctr[0] += 1
            return b

        slot_ctr = [0]

        def ring_load(parts):
            s = slot_ctr[0] % NSLOT
            slot_ctr[0] += 1
            views = []
            off = 0
            keys = []
            for i, src in enumerate(parts):
                K, n = src.shape
                kc = K // 128
                view = RING[:, s, off:off + kc * n].rearrange("p (k n) -> p k n", n=n)
                wk = [("ring", s, 0), ("ring", s, 1)] if i == 0 else [("ring", s, i)]
                P.dma("pool", lambda e, view=view, src=src: e.dma_start(
                    out=view, in_=src.rearrange("(k p) n -> p k n", p=128)),
                    "ring%d" % s, writes=wk, skip_own=True)
                views.append(view)
                keys.append(("ring", s, i))
                off += kc * n
            assert off <= SLOT
            return views, keys

        P.dma("sp", lambda e: e.dma_start(out=PPS[:], in_=pp_d), "c0", writes=["pp"])
        P.dma("sp", lambda e: e.dma_start(out=MASK[:], in_=mask_d), "c1", writes=["mask"])
        P.dma("sp", lambda e: e.dma_start(out=WSF[:, 0, :], in_=ident_d), "c2", writes=["wsf"])
        P.dma("sp", lambda e: e.dma_start(out=FLAG[:], in_=flag_d), "c3", writes=["flag"])
        P.dma("sp", lambda e: e.dma_start(out=PASTF[:], in_=pastT), "c4", writes=["pastf"])
        P.op("dve", lambda e: e.memset(ONES[:], 1.0), writes=["ones"])
        P.op("dve", lambda e: e.memset(EPS[:, 0:1], RMS_EPS), writes=["eps0"])
        P.op("dve", lambda e: e.memset(EPS[:, 1:2], LN_EPS), writes=["eps1"])
        P.op("dve", lambda e: e.tensor_copy(out=IDENTB[:], in_=WSF[:, 0, :]), reads=["wsf"], writes=["identb"])

        def x_keys(tt):
            return [("x", kc, tt) for kc in range(KC)]

        def h_keys(tt):
            return [("h", kc, tt) for kc in range(KC)]

        def rmsnorm(tiles, gname, gidx, out_fn, out_keys_fn):
            banks = {}

            def stage_a(tt):
                c0, n = tiles[tt]
                par = tt % 2
                for kc in range(KC):
                    P.op("act", lambda e, kc=kc, c0=c0, n=n, par=par: e.activation(
                        out=SQ[:, par, kc, 0:n], in_=xA[:, kc, c0:c0 + n], func=AF.Square),
                        reads=[("x", kc, tt)], writes=[("sq", par, kc)])
                b = newbank()
                banks[tt] = b
                P.op("pe", [lambda e, kc=kc, n=n, par=par, b=b: e.matmul(
                    ps[:, b, 0:n], lhsT=ONES[:, :], rhs=SQ[:, par, kc, 0:n], start=(kc == 0), stop=(kc == KC - 1))
                    for kc in range(KC)],
                    reads=["ones"] + [("sq", par, kc) for kc in range(KC)], writes=[("ps", b)])

            def stage_b(tt):
                c0, n = tiles[tt]
                par = tt % 2
                b = banks[tt]
                P.op("act", lambda e, n=n, par=par, b=b: e.activation(
                    out=FT[:, par, 0:n], in_=ps[:, b, 0:n], func=AF.Sqrt, scale=1.0 / D, bias=EPS[:, 0:1]),
                    reads=[("ps", b), "eps0"], writes=[("ft", par)])
                P.op("dve", lambda e, n=n, par=par: e.reciprocal(out=FT[:, 2 + par, 0:n], in_=FT[:, par, 0:n]),
                     reads=[("ft", par)], writes=[("ft", 2 + par)])
                for kc in range(KC):
                    P.op("dve", lambda e, kc=kc, c0=c0, n=n, par=par, tt=tt: e.scalar_tensor_tensor(
                        out=out_fn(kc, tt, c0, n), in0=xA[:, kc, c0:c0 + n], scalar=pcol(gname, gidx * KC + kc),
                        in1=FT[:, 2 + par, 0:n], op0=ALU.mult, op1=ALU.mult),
                        reads=[("x", kc, tt), ("ft", 2 + par), "pp"], writes=[out_keys_fn(kc, tt)])
            nt = len(tiles)
            stage_a(0)
            for tt in range(nt):
                if tt + 1 < nt:
                    stage_a(tt + 1)
                stage_b(tt)

        def norm_to_h(tiles, gname, gidx):
            rmsnorm(tiles, gname, gidx, lambda kc, tt, c0, n: hA[:, kc, c0:c0 + n], lambda kc, tt: ("h", kc, tt))

        aux_snap = {"pe": 0, "act": 0, "dve": 0}

        def snap_aux():
            for e2 in aux_snap:
                aux_snap[e2] = P.cnt[e2]

        def ffn(layer, tiles):
            P.barrier()
            norm_to_h(tiles, "ffng", layer)
            for mb in range(11):
                (wg, wu), keys = ring_load([w_gate[layer, :, mb * 256:(mb + 1) * 256],
                                            w_up[layer, :, mb * 256:(mb + 1) * 256]])
                for tt, (c0, n) in enumerate(tiles):
                    for j in range(2):
                        mc = mb * 2 + j
                        par = (tt * 2 + j) % 2
                        bg = newbank()
                        bu = newbank()
                        for (w_, b_) in ((wg, bg), (wu, bu)):
                            P.op("pe", [lambda e, kc=kc, w_=w_, b_=b_, c0=c0, n=n, j=j: e.matmul(
                                ps[:, b_, 0:n], lhsT=w_[:, kc, j * 128:(j + 1) * 128], rhs=hA[:, kc, c0:c0 + n],
                                start=(kc == 0), stop=(kc == KC - 1)) for kc in range(KC)],
                                reads=keys + h_keys(tt), writes=[("ps", b_)])
                        P.op("act", lambda e, bg=bg, n=n, par=par: e.activation(
                            out=FT[:, 4 + par, 0:n], in_=ps[:, bg, 0:n], func=AF.Silu),
                            reads=[("ps", bg)], writes=[("ft", 4 + par)])
                        P.op("dve", lambda e, bu=bu, n=n, par=par, mc=mc, c0=c0: e.tensor_tensor(
                            out=A_[:, mc, c0:c0 + n], in0=ps[:, bu, 0:n], in1=FT[:, 4 + par, 0:n], op=ALU.mult),
                            reads=[("ps", bu), ("ft", 4 + par)], writes=[("a", mc, tt)])
            for m in range(KC):
                (sW,), kW = ring_load([w_down[layer, :, m * 128:(m + 1) * 128]])
                for tt, (c0, n) in enumerate(tiles):
                    b = newbank()
                    P.op("pe", [lambda e, kc=kc, b=b, c0=c0, n=n, sW=sW: e.matmul(
                        ps[:, b, 0:n], lhsT=sW[:, kc, :],
                        rhs=A_[:, kc, c0:c0 + n], start=(kc == 0), stop=(kc == 21)) for kc in range(22)],
                        reads=kW + [("a", kc, tt) for kc in range(22)], writes=[("ps", b)])
                    P.op("dve", lambda e, b=b, m=m, c0=c0, n=n: e.tensor_tensor(
                        out=xA[:, m, c0:c0 + n], in0=ps[:, b, 0:n], in1=xA[:, m, c0:c0 + n], op=ALU.add),
                        reads=[("ps", b), ("x", m, tt)], writes=[("x", m, tt)])

        def sgu(l, layer, tiles, has_sample):
            P.barrier()
            for e2 in ("pe", "act", "dve"):
                if aux_snap[e2] > 0:
                    P.ops["sp"].append(lambda e, h_=P.sem[e2], v_=aux_snap[e2]: e.wait_ge(h_, v_))
            if "o_c" in P.dma_sems:
                h_, v_ = P.dma_sems["o_c"]
                P.ops["sp"].append(lambda e, h_=h_, v_=v_: e.wait_ge(h_, v_))
            P.dma("sp", lambda e: e.dma_start(out=WSF[:], in_=wsT_d[l]), "c2", writes=["wsf"])
            P.dma("sp", lambda e: e.dma_start(out=BSB[:].rearrange("p h i -> p (h i)"),
                                              in_=bs_d[l].partition_broadcast(128)), "c5", writes=["bsb"])

            def prep_bias(nb):
                bp = nb % 2
                P.dma("sp", lambda e, nb=nb: e.dma_start(out=BRF[0:1, :], in_=binv_d[l:l + 1, nb * 512:(nb + 1) * 512]),
                      "c6", writes=["brf"])
                P.op("dve", lambda e, bp=bp: e.tensor_copy(out=BRH2[0:1, bp, :], in_=BRF[0:1, :]),
                     reads=["brf"], writes=[("brh", bp)])
                P.op("dve", lambda e, bp=bp: e.tensor_tensor(out=BRL32[0:1, :], in0=BRF[0:1, :], in1=BRH2[0:1, bp, :],
                                                             op=ALU.subtract),
                     reads=["brf", ("brh", bp)], writes=["brl32"])
                P.op("dve", lambda e: e.tensor_copy(out=BRLO[0:1, :], in_=BRL32[0:1, :]),
                     reads=["brl32"], writes=["brlo"])
                P.dma("sp", lambda e, bp=bp: e.dma_start(out=BRH2[1:2, bp, :], in_=BRLO[0:1, :]), "c7",
                      reads=["brlo"], writes=[("brh1", bp)])
            prep_bias(0)
            for hd in range(4):
                P.op("dve", lambda e, hd=hd: e.tensor_tensor(out=WSM[:, hd, :], in0=WSF[:, hd, :], in1=MASK[:],
                                                             op=ALU.mult),
                     reads=["wsf", "mask"], writes=[("wsm", hd)])
            norm_to_h(tiles, "mixg", layer)
            chunk_list = []
            chunk_tile = {}
            for tt_, (c0_, n_) in enumerate(tiles):
                for cc_ in range(c0_, c0_ + (n_ // 128) * 128, 128):
                    chunk_list.append((cc_ // 128, cc_, 128))
                    chunk_tile[cc_ // 128] = tt_
            cmin, cmax1 = chunk_list[0][0], chunk_list[-1][0] + 1
            if has_sample:
                chunk_list.append((9, 1152, 32))
                chunk_tile[9] = 2
            for nb in range(6):
                (wv,), kv = ring_load([w_in[l, :, DSGU + nb * 512:DSGU + (nb + 1) * 512]])
                bp = nb % 2
                if nb + 1 < 6:
                    prep_bias(nb + 1)
                for (ci, cc0, rows) in chunk_list:
                    b = newbank()
                    hk = h_keys(chunk_tile[ci])
                    P.op("pe", [lambda e, kc=kc, b=b, cc0=cc0, rows=rows, wv=wv: e.matmul(
                        ps[0:rows, b, :], lhsT=hA[:, kc, cc0:cc0 + rows], rhs=wv[:, kc, :],
                        start=(kc == 0), stop=False) for kc in range(KC)]
                        + [lambda e, b=b, rows=rows, bp=bp: e.matmul(
                            ps[0:rows, b, :], lhsT=ONES[0:2, 0:rows], rhs=BRH2[0:2, bp, :], start=False, stop=True)],
                        reads=kv + hk + ["ones", ("brh", bp), ("brh1", bp)], writes=[("ps", b)])
                    if ci < 9:
                        dst = VN[:, ci, nb * 512:(nb + 1) * 512]
                        dkey = ("vn", ci, nb)
                    else:
                        dst = VS[0:32, nb * 512:(nb + 1) * 512]
                        dkey = ("vs", nb)
                    P.op("act", lambda e, b=b, rows=rows, dst=dst: e.activation(
                        out=dst, in_=ps[0:rows, b, :], func=AF.Gelu_apprx_tanh),
                        reads=[("ps", b)], writes=[dkey])
                    P.op("dve", lambda e, ci=ci, nb=nb, rows=rows, dst=dst: e.bn_stats(
                        out=STATS[0:rows, ci, nb, :], in_=dst), reads=[dkey], writes=[("stats", ci, nb)])
            for (ci, cc0, rows) in chunk_list:
                P.op("dve", lambda e, ci=ci, rows=rows: e.bn_aggr(
                    out=MV[0:rows, ci, :], in_=STATS[0:rows, ci, :, :].rearrange("p a b -> p (a b)")),
                    reads=[("stats", ci, nb) for nb in range(6)], writes=[("mv", ci)])
            groups = [(cmin, cmax1, 128)] + ([(9, 10, 32)] if has_sample else [])
            for (g0, g1, rows) in groups:
                gk = ("scg", g0)
                P.op("dve", lambda e, g0=g0, g1=g1, rows=rows: e.tensor_scalar(
                    out=SC[0:rows, 0, g0:g1], in0=MV[0:rows, g0:g1, 1], scalar1=LN_EPS, scalar2=None, op0=ALU.add),
                    reads=[("mv", ci) for ci in range(g0, g1)], writes=[gk + (0,)])
                P.op("act", lambda e, g0=g0, g1=g1, rows=rows: e.activation(
                    out=SC[0:rows, 1, g0:g1], in_=SC[0:rows, 0, g0:g1], func=AF.Sqrt),
                    reads=[gk + (0,)], writes=[gk + (1,)])
                P.op("dve", lambda e, g0=g0, g1=g1, rows=rows: e.reciprocal(
                    out=SC[0:rows, 2, g0:g1], in_=SC[0:rows, 1, g0:g1]), reads=[gk + (1,)], writes=[gk + (2,)])
                P.op("dve", lambda e, g0=g0, g1=g1, rows=rows: e.scalar_tensor_tensor(
                    out=SC[0:rows, 3, g0:g1], in0=MV[0:rows, g0:g1, 0], scalar=-1.0, in1=SC[0:rows, 2, g0:g1],
                    op0=ALU.mult, op1=ALU.mult),
                    reads=[("mv", ci) for ci in range(g0, g1)] + [gk + (2,)], writes=[gk + (3,)])
            for ci in range(cmin, cmax1):
                P.op("dve", lambda e, ci=ci: e.tensor_scalar(
                    out=VN[:, ci, :], in0=VN[:, ci, :], scalar1=SC[:, 2, ci:ci + 1], scalar2=SC[:, 3, ci:ci + 1],
                    op0=ALU.mult, op1=ALU.add),
                    reads=[("vn", ci, nb) for nb in range(6)] + [("scg", cmin, 2), ("scg", cmin, 3)],
                    writes=[("vnn", ci)])
            if has_sample:
                P.op("dve", lambda e: e.tensor_scalar(
                    out=VN[0:32, 9, :], in0=VS[0:32, :], scalar1=SC[0:32, 2, 9:10], scalar2=SC[0:32, 3, 9:10],
                    op0=ALU.mult, op1=ALU.add),
                    reads=[("vs", nb) for nb in range(6)] + [("scg", 9, 2), ("scg", 9, 3)], writes=[("vnn", 9)])
                P.op("dve", lambda e: e.tensor_scalar(
                    out=VS[0:32, :], in0=VS[0:32, :], scalar1=SC[0:32, 2, 9:10], scalar2=SC[0:32, 3, 9:10],
                    op0=ALU.mult, op1=ALU.add),
                    reads=[("vs", nb) for nb in range(6)] + [("scg", 9, 2), ("scg", 9, 3), ("vnn", 9)],
                    writes=[("vs", nb) for nb in range(6)])
            def sample_affine(nb):
                P.dma("sp", lambda e, nb=nb: e.dma_start(
                    out=GB[0:32, 0, :], in_=lngr_d[l, nb * 512:(nb + 1) * 512].partition_broadcast(32)),
                    "c8", writes=["gb0", ("ft", 0), ("ft", 1)])
                P.dma("sp", lambda e, nb=nb: e.dma_start(
                    out=GB[0:32, 1, :], in_=lnbr_d[l, nb * 512:(nb + 1) * 512].partition_broadcast(32)),
                    "c9", writes=["gb1", ("ft", 1), ("ft", 2)])
                P.op("dve", lambda e, nb=nb: e.tensor_tensor(
                    out=VS[0:32, nb * 512:(nb + 1) * 512], in0=VS[0:32, nb * 512:(nb + 1) * 512],
                    in1=GB[0:32, 0, :], op=ALU.mult), reads=[("vs", nb), "gb0", ("ft", 0), ("ft", 1)],
                    writes=[("vs", nb)])
                P.op("dve", lambda e, nb=nb: e.tensor_tensor(
                    out=VS[0:32, nb * 512:(nb + 1) * 512], in0=VS[0:32, nb * 512:(nb + 1) * 512],
                    in1=GB[0:32, 1, :], op=ALU.add), reads=[("vs", nb), "gb1", ("ft", 1), ("ft", 2)],
                    writes=[("vs", nb)])
            if P.cnt["pe"] > 0:
                P.ops["dve"].append(lambda e, h_=P.sem["pe"], v_=P.cnt["pe"]: e.wait_ge(h_, v_))
            brs = newbank()
            P.op("pe", [lambda e, hd=hd, brs=brs: e.matmul(
                ps[:, brs, hd * 128:(hd + 1) * 128], lhsT=ONES[:, :], rhs=WSM[:, hd, :], start=True, stop=True)
                for hd in range(4)],
                reads=["ones"] + [("wsm", hd) for hd in range(4)], writes=[("ps", brs)])
            RSB = FT[:, 4:6, :].rearrange("p a n -> p (a n)")
            P.op("act", lambda e, brs=brs: e.activation(out=RSB[:, 0:512], in_=ps[:, brs, :], func=AF.Copy),
                 reads=[("ps", brs)], writes=[("ft", 4), ("ft", 5)])
            for dc in range(24):
                hd = dc // 6
                P.op("dve", lambda e, dc=dc, hd=hd: e.scalar_tensor_tensor(
                    out=CB[:, dc, :], in0=RSB[:, hd * 128:(hd + 1) * 128], scalar=pcol("lnb", l * 24 + dc),
                    in1=BSB[:, hd, :], op0=ALU.mult, op1=ALU.add),
                    reads=[("ft", 4), ("ft", 5), "bsb", "pp"], writes=[("cb", dc)])
            for hd in range(4):
                wus = []
                kus = []
                for s in range(2):
                    (w_,), k_ = ring_load([w_in[l, :, hd * 768 + s * 384: hd * 768 + (s + 1) * 384]])
                    wus.append(w_)
                    kus.append(k_)
                def u_groups(mc):
                    dc = hd * 6 + mc
                    s, off = mc // 3, (mc % 3) * 128
                    for tt, (c0, n) in enumerate(tiles):
                        b = newbank()
                        P.op("pe", [lambda e, kc=kc, b=b, c0=c0, n=n, w_=wus[s], off=off: e.matmul(
                            ps[:, b, 0:n], lhsT=w_[:, kc, off:off + 128], rhs=hA[:, kc, c0:c0 + n],
                            start=(kc == 0), stop=(kc == KC - 1)) for kc in range(KC)],
                            reads=kus[s] + h_keys(tt), writes=[("ps", b)])
                        P.op("act", lambda e, b=b, mc=mc, c0=c0, n=n, dc=dc: e.activation(
                            out=UG[:, mc, c0:c0 + n], in_=ps[:, b, 0:n], func=AF.Gelu_apprx_tanh,
                            bias=pcol("binu", l * 24 + dc)),
                            reads=[("ps", b), "pp"], writes=[("ug", mc, tt)])

                def mix_one(mc, tt):
                    c0, n = tiles[tt]
                    nch = n // 128
                    nm = nch * 128
                    par = tt % 2
                    if True:
                        dc = hd * 6 + mc
                        b = newbank()
                        mms = []
                        rk = [("wsm", hd)]
                        for jj in range(nch):
                            ci = c0 // 128 + jj
                            mms.append(lambda e, b=b, jj=jj, ci=ci, dc=dc, hd=hd: e.matmul(
                                ps[:, b, jj * 128:(jj + 1) * 128], lhsT=VN[:, ci, dc * 128:(dc + 1) * 128],
                                rhs=WSM[:, hd, :], start=True, stop=True))
                            rk.append(("vnn", ci))
                        if n > nm:
                            mms.append(lambda e, b=b, dc=dc, hd=hd: e.matmul(
                                ps[:, b, nm:nm + 32], lhsT=VN[0:32, 9, dc * 128:(dc + 1) * 128],
                                rhs=WSM[0:32, hd, 0:32], start=True, stop=True))
                            rk.append(("vnn", 9))
                        P.op("pe", mms, reads=rk, writes=[("ps", b)])
                        P.op("dve", lambda e, b=b, par=par, dc=dc: e.scalar_tensor_tensor(
                            out=FT[:, 4 + par, 0:nm].rearrange("p (a i) -> p a i", i=128),
                            in0=ps[:, b, 0:nm].rearrange("p (a i) -> p a i", i=128),
                            scalar=pcol("lng", l * 24 + dc), in1=CB[:, dc:dc + 1, :].to_broadcast([128, nch, 128]),
                            op0=ALU.mult, op1=ALU.add),
                            reads=[("ps", b), ("cb", dc), "pp"], writes=[("ft", 4 + par)])
                        if n > nm:
                            P.op("dve", lambda e, b=b, par=par, dc=dc: e.scalar_tensor_tensor(
                                out=FT[:, 4 + par, nm:nm + 32], in0=ps[:, b, nm:nm + 32],
                                scalar=pcol("lng", l * 24 + dc), in1=CB[:, dc, 0:32],
                                op0=ALU.mult, op1=ALU.add),
                                reads=[("ps", b), ("cb", dc), "pp"], writes=[("ft", 4 + par, "s")])
                        P.op("dve", lambda e, par=par, mc=mc, c0=c0, n=n: e.tensor_tensor(
                            out=UG[:, mc, c0:c0 + n], in0=FT[:, 4 + par, 0:n], in1=UG[:, mc, c0:c0 + n], op=ALU.mult),
                            reads=[("ft", 4 + par), ("ft", 4 + par, "s"), ("ug", mc, tt)], writes=[("ug", mc, tt)])

                for mc in range(6):
                    u_groups(mc)
                    for tt in range(len(tiles)):
                        mix_one(mc, tt)
                    if has_sample and mc == 2:
                        sample_affine(hd)
                    if has_sample and mc == 4 and hd < 2:
                        sample_affine(4 + hd)
                wos = []
                kos = []
                for s in range(2):
                    (w_,), k_ = ring_load([w_out[l, hd * 768:(hd + 1) * 768, s * 512:(s + 1) * 512]])
                    wos.append(w_)
                    kos.append(k_)
                for m in range(KC):
                    s, off = m // 4, (m % 4) * 128
                    for tt, (c0, n) in enumerate(tiles):
                        b = newbank()
                        P.op("pe", [lambda e, kc=kc, b=b, c0=c0, n=n, w_=wos[s], off=off: e.matmul(
                            ps[:, b, 0:n], lhsT=w_[:, kc, off:off + 128], rhs=UG[:, kc, c0:c0 + n],
                            start=(kc == 0), stop=(kc == 5)) for kc in range(6)],
                            reads=kos[s] + [("ug", kc, tt) for kc in range(6)], writes=[("ps", b)])
                        sc = pcol("bout", l * 8 + m) if hd == 0 else 0.0
                        P.op("dve", lambda e, b=b, m=m, c0=c0, n=n, sc=sc: e.scalar_tensor_tensor(
                            out=xA[:, m, c0:c0 + n], in0=ps[:, b, 0:n], scalar=sc, in1=xA[:, m, c0:c0 + n],
                            op0=ALU.add, op1=ALU.add),
                            reads=[("ps", b), ("x", m, tt), "pp"], writes=[("x", m, tt)])
            if has_sample:
                P.dma("sp", lambda e: e.dma_start(out=vout[l], in_=VS[0:32, :]), "o_v",
                      reads=[("vs", nb) for nb in range(6)])
                h_, v_ = P.dma_sems["o_v"]
                for eng in ("act", "dve"):
                    P.ops[eng].append(lambda e, h_=h_, v_=v_: e.wait_ge(h_, v_))

        def conv(l, layer, tiles_c1, tiles, st):
            nmain = 1152
            def build_diag(kc, ks):
                dp = kc % 2
                for k in ks:
                    P.op("act", lambda e, kc=kc, k=k, dp=dp: e.activation(
                        out=DIAG[:, dp, k, :], in_=IDENTB[:, :], func=AF.Copy,
                        scale=pcol("wdw", l * 248 + kc * 31 + k)),
                        reads=["identb", "pp"], writes=[("diag", dp, k)])
            for e2 in ("pe", "dve"):
                if aux_snap[e2] > 0:
                    P.ops["act"].append(lambda e, h_=P.sem[e2], v_=aux_snap[e2]: e.wait_ge(h_, v_))
            if "o_c" in P.dma_sems:
                h_, v_ = P.dma_sems["o_c"]
                P.ops["act"].append(lambda e, h_=h_, v_=v_: e.wait_ge(h_, v_))
            build_diag(0, range(NDVE, 31))
            P.barrier()
            norm_to_h(tiles_c1, "mixg", layer)
            if st == 0:
                P.op("dve", lambda e: e.memset(PD[:, :, 0:30], 0.0), writes=["pd_past"])
            else:
                P.op("act", lambda e: e.activation(out=PD[:, :, 0:30], in_=TAIL[:, l, :, :], func=AF.Copy),
                     reads=[("tail", l)], writes=["pd_past"])
                P.op("act", lambda e: e.activation(out=PD[:, :, 1182:1212], in_=PASTF[:, l, :, :], func=AF.Copy),
                     reads=["pastf"], writes=["pd_spast"])
            for m2 in range(4):
                (wa, wb), keys = ring_load([w_pw1[l, :, m2 * 256:(m2 + 1) * 256],
                                            w_pw1[l, :, D + m2 * 256:D + (m2 + 1) * 256]])
                for tt, (c0, n) in enumerate(tiles_c1):
                    for j in range(2):
                        m = m2 * 2 + j
                        par = (tt * 2 + j) % 2
                        nm = (n // 128) * 128
                        ba = newbank()
                        bb = newbank()
                        for (w_, b_) in ((wa, ba), (wb, bb)):
                            P.op("pe", [lambda e, kc=kc, w_=w_, b_=b_, c0=c0, n=n, j=j: e.matmul(
                                ps[:, b_, 0:n], lhsT=w_[:, kc, j * 128:(j + 1) * 128], rhs=hA[:, kc, c0:c0 + n],
                                start=(kc == 0), stop=(kc == KC - 1)) for kc in range(KC)],
                                reads=keys + h_keys(tt), writes=[("ps", b_)])
                        P.op("act", lambda e, bb=bb, n=n, par=par, m=m: e.activation(
                            out=FT[:, 4 + par, 0:n], in_=ps[:, bb, 0:n], func=AF.Sigmoid,
                            bias=pcol("bpw1", l * 16 + 8 + m)),
                            reads=[("ps", bb), "pp"], writes=[("ft", 4 + par)])
                        P.op("dve", lambda e, ba=ba, par=par, m=m, c0=c0: e.scalar_tensor_tensor(
                            out=PD[:, m, 30 + c0:30 + c0 + nm], in0=ps[:, ba, 0:nm],
                            scalar=pcol("bpw1", l * 16 + m), in1=FT[:, 4 + par, 0:nm], op0=ALU.add, op1=ALU.mult),
                            reads=[("ps", ba), ("ft", 4 + par), "pp"], writes=[("pd", m, tt)])
                        if st == 0 and tt == 0:
                            P.op("dve", lambda e, m=m, c0=c0: e.tensor_scalar(
                                out=PD[:, m, 30 + c0:30 + 256], in0=PD[:, m, 30 + c0:30 + 256], scalar1=FLAG[:, 0:1],
                                scalar2=None, op0=ALU.mult),
                                reads=[("pd", m, tt), "flag"], writes=[("pd", m, tt)])
                        if n > nm:
                            P.op("dve", lambda e, ba=ba, par=par, m=m: e.scalar_tensor_tensor(
                                out=PD[:, m, 1212:1244], in0=ps[:, ba, nm:nm + 32],
                                scalar=pcol("bpw1", l * 16 + m), in1=FT[:, 4 + par, nm:nm + 32],
                                op0=ALU.add, op1=ALU.mult),
                                reads=[("ps", ba), ("ft", 4 + par), "pp"], writes=[("pds", m)])
                            P.op("dve", lambda e, ba=ba, par=par, m=m: e.scalar_tensor_tensor(
                                out=CST[:, m, :], in0=ps[:, ba, nm - 30:nm + 32],
                                scalar=pcol("bpw1", l * 16 + m), in1=FT[:, 4 + par, nm - 30:nm + 32],
                                op0=ALU.add, op1=ALU.mult),
                                reads=[("ps", ba), ("ft", 4 + par), "pp"], writes=[("cst", m)])
            if st == 0:
                P.op("act", lambda e: e.activation(out=TAIL[:, l, :, :], in_=PD[:, :, 1152:1182], func=AF.Copy),
                     reads=[("pd", m, 2) for m in range(KC)], writes=[("tail", l)])
            else:
                P.dma("sp", lambda e: e.dma_start(out=convT[:, l, 0, :, :], in_=CST[:, :, 0:30]), "o_c",
                      reads=[("cst", m) for m in range(KC)])
                P.dma("sp", lambda e: e.dma_start(out=convT[:, l, 1, :, :], in_=CST[:, :, 32:62]), "o_c",
                      reads=[("cst", m) for m in range(KC)])
            ln_banks = {}

            def ln_cs(tt):
                c0, n = tiles[tt]
                for kc in range(KC):
                    P.op("dve", lambda e, kc=kc, c0=c0, n=n: e.tensor_copy(
                        out=SQ[:, 0, kc, 0:n], in_=CC[:, kc, c0:c0 + n]),
                        reads=[("cc", kc, tt)], writes=[("sq", 0, kc)])
                    P.op("act", lambda e, kc=kc, c0=c0, n=n: e.activation(
                        out=SQ[:, 1, kc, 0:n], in_=CC[:, kc, c0:c0 + n], func=AF.Square),
                        reads=[("cc", kc, tt)], writes=[("sq", 1, kc)])

            def ln_stats(tt):
                c0, n = tiles[tt]
                b1 = newbank()
                b2 = newbank()
                ln_banks[tt] = (b1, b2)
                for (q, b_) in ((0, b1), (1, b2)):
                    P.op("pe", [lambda e, kc=kc, n=n, q=q, b_=b_: e.matmul(
                        ps[:, b_, 0:n], lhsT=ONES[:, :], rhs=SQ[:, q, kc, 0:n], start=(kc == 0), stop=(kc == KC - 1))
                        for kc in range(KC)],
                        reads=["ones"] + [("sq", q, kc) for kc in range(KC)], writes=[("ps", b_)])

            def ln_b1(tt):
                c0, n = tiles[tt]
                b1, b2 = ln_banks[tt]
                P.op("act", lambda e, b1=b1, n=n: e.activation(
                    out=FT[:, 4, 0:n], in_=ps[:, b1, 0:n], func=AF.Copy, scale=1.0 / D),
                    reads=[("ps", b1)], writes=[("ft", 4)])
                P.op("act", lambda e, b1=b1, n=n: e.activation(
                    out=FT[:, 5, 0:n], in_=ps[:, b1, 0:n], func=AF.Square, scale=1.0 / D),
                    reads=[("ps", b1)], writes=[("ft", 5)])
                P.op("dve", lambda e, b2=b2, n=n: e.scalar_tensor_tensor(
                    out=FT[:, 0, 0:n], in0=ps[:, b2, 0:n], scalar=1.0 / D, in1=FT[:, 5, 0:n],
                    op0=ALU.mult, op1=ALU.subtract),
                    reads=[("ps", b2), ("ft", 5)], writes=[("ft", 0)])
                P.op("act", lambda e, n=n: e.activation(
                    out=FT[:, 1, 0:n], in_=FT[:, 0, 0:n], func=AF.Sqrt, bias=EPS[:, 1:2]),
                    reads=[("ft", 0), "eps1"], writes=[("ft", 1)])
                P.op("dve", lambda e, n=n: e.reciprocal(out=FT[:, 2, 0:n], in_=FT[:, 1, 0:n]),
                     reads=[("ft", 1)], writes=[("ft", 2)])

            def ln_b2(tt):
                c0, n = tiles[tt]
                for kc in range(KC):
                    P.op("dve", lambda e, kc=kc, c0=c0, n=n: e.tensor_tensor(
                        out=CC[:, kc, c0:c0 + n], in0=CC[:, kc, c0:c0 + n], in1=FT[:, 4, 0:n], op=ALU.subtract),
                        reads=[("cc", kc, tt), ("ft", 4), ("sq", 0, kc), ("sq", 1, kc)], writes=[("cc", kc, tt)])
                for kc in range(KC):
                    P.op("dve", lambda e, kc=kc, c0=c0, n=n: e.tensor_tensor(
                        out=CC[:, kc, c0:c0 + n], in0=CC[:, kc, c0:c0 + n], in1=FT[:, 2, 0:n], op=ALU.mult),
                        reads=[("cc", kc, tt), ("ft", 2)], writes=[("cc", kc, tt)])
                    P.op("act", lambda e, kc=kc, c0=c0, n=n: e.activation(
                        out=hA[:, kc, c0:c0 + n], in_=CC[:, kc, c0:c0 + n], func=AF.Silu,
                        scale=pcol("clng", l * 8 + kc), bias=pcol("clnb", l * 8 + kc)),
                        reads=[("cc", kc, tt), "pp"], writes=[("h", kc, tt)])

            def ndve_of(kc):
                return NDVE if kc < KC - 1 else 0

            def kparts_of(kc):
                nd = ndve_of(kc)
                ks = list(range(nd, 31))
                a = len(ks) // 3
                return [ks[0:a], ks[a:2 * a], ks[2 * a:]]
            for kc in range(KC):
                dp = kc % 2
                nd = ndve_of(kc)
                for tt, (c0, n) in enumerate(tiles):
                    nm = (n // 128) * 128
                    if kc + 1 < KC:
                        build_diag(kc + 1, kparts_of(kc + 1)[tt])
                    b = newbank()
                    mms = [lambda e, k=k, b=b, kc=kc, dp=dp, c0=c0, nm=nm, nd=nd: e.matmul(
                        ps[:, b, 0:nm], lhsT=DIAG[:, dp, k, :], rhs=PD[:, kc, c0 + k:c0 + k + nm],
                        start=(k == nd), stop=(k == 30)) for k in range(nd, 31)]
                    rk = [("diag", dp, k) for k in range(nd, 31)] + [("pd", kc, tt), "pd_past"]
                    if tt > 0:
                        rk.append(("pd", kc, tt - 1))
                    if n > nm:
                        mms += [lambda e, k=k, b=b, kc=kc, dp=dp, nm=nm, nd=nd: e.matmul(
                            ps[:, b, nm:nm + 32], lhsT=DIAG[:, dp, k, :], rhs=PD[:, kc, 1182 + k:1182 + k + 32],
                            start=(k == nd), stop=(k == 30)) for k in range(nd, 31)]
                        rk += [("pds", kc), "pd_spast"]
                    P.op("pe", mms, reads=rk, writes=[("ps", b)])
                    P.op("act", lambda e, b=b, kc=kc, c0=c0, n=n: e.activation(
                        out=CC[:, kc, c0:c0 + n], in_=ps[:, b, 0:n], func=AF.Identity, bias=pcol("bdw", l * 8 + kc)),
                        reads=[("ps", b), "pp"], writes=[("cc", kc, tt)])
                    if kc == KC - 1:
                        if tt > 0:
                            ln_stats(tt - 1)
                            ln_b1(tt - 1)
                        ln_cs(tt)
                        if tt > 0:
                            ln_b2(tt - 1)
                for k in range(nd):
                    for tt, (c0, n) in enumerate(tiles):
                        nm = (n // 128) * 128
                        rk = [("cc", kc, tt), ("pd", kc, tt), "pd_past", "pp"]
                        if tt > 0:
                            rk.append(("pd", kc, tt - 1))
                        P.op("dve", lambda e, k=k, kc=kc, c0=c0, nm=nm: e.scalar_tensor_tensor(
                            out=CC[:, kc, c0:c0 + nm], in0=PD[:, kc, c0 + k:c0 + k + nm],
                            scalar=pcol("wdw", l * 248 + kc * 31 + k), in1=CC[:, kc, c0:c0 + nm],
                            op0=ALU.mult, op1=ALU.add), reads=rk, writes=[("cc", kc, tt)])
                        if n > nm:
                            P.op("dve", lambda e, k=k, kc=kc, c0=c0, nm=nm: e.scalar_tensor_tensor(
                                out=CC[:, kc, c0 + nm:c0 + nm + 32], in0=PD[:, kc, 1182 + k:1182 + k + 32],
                                scalar=pcol("wdw", l * 248 + kc * 31 + k), in1=CC[:, kc, c0 + nm:c0 + nm + 32],
                                op0=ALU.mult, op1=ALU.add),
                                reads=[("pds", kc), "pd_spast", "pp"],
                                writes=[("cc", kc, tt)])
            pw2w = []
            pw2k = []
            for s in range(2):
                (w_,), k_ = ring_load([w_pw2[l, :, s * 512:(s + 1) * 512]])
                pw2w.append(w_)
                pw2k.append(k_)

            def pw2_tile(tt):
                c0, n = tiles[tt]
                for m in range(KC):
                    s, j = m // 4, m % 4
                    b = newbank()
                    P.op("pe", [lambda e, kc=kc, b=b, w_=pw2w[s], j=j: e.matmul(
                        ps[:, b, 0:n], lhsT=w_[:, kc, j * 128:(j + 1) * 128], rhs=hA[:, kc, c0:c0 + n],
                        start=(kc == 0), stop=(kc == KC - 1)) for kc in range(KC)],
                        reads=pw2k[s] + h_keys(tt), writes=[("ps", b)])
                    P.op("dve", lambda e, b=b, m=m: e.scalar_tensor_tensor(
                        out=xA[:, m, c0:c0 + n], in0=ps[:, b, 0:n], scalar=pcol("bpw2", l * 8 + m),
                        in1=xA[:, m, c0:c0 + n], op0=ALU.add, op1=ALU.add),
                        reads=[("ps", b), ("x", m, tt), "pp"], writes=[("x", m, tt)])

            nt = len(tiles)
            ln_stats(nt - 1)
            ln_b1(nt - 1)
            pw2_tile(0)
            ln_b2(nt - 1)
            for tt in range(1, nt):
                pw2_tile(tt)

        for st in range(nst):
            g0, T = ST_COLS[st]
            tiles = [(0, 384), (384, 384), (768, T - 768)]
            P.barrier()
            for tt, (c0, n) in enumerate(tiles):
                for kc in range(KC):
                    P.dma("sp", lambda e, c0=c0, n=n, g0=g0, kc=kc: e.dma_start(
                        out=xA[:, kc, c0:c0 + n], in_=xT[:, kc, g0 + c0:g0 + c0 + n]),
                        "xin%d_%d" % (tt, kc), writes=[("x", kc, tt)])
            if st == 0:
                t_noh0 = [(128, 256), (384, 384), (768, 384)]
                t_main = [(256, 128), (384, 384), (768, 384)]
                plan = [(tiles, tiles, tiles), (tiles, t_noh0, t_noh0), (t_noh0, t_noh0, t_noh0),
                        (t_noh0, t_main, t_main)]
                t_final = t_main
            else:
                plan = [(tiles, tiles, tiles)] * 4
                t_final = tiles
            for layer in range(nlayers):
                l = layer // 2
                t_a, t_b, t_f = plan[layer]
                if layer % 2 == 0:
                    sgu(l, layer, t_a, st == 1)
                else:
                    conv(l, layer, t_a, t_b, st)
                snap_aux()
                ffn(layer, t_f)
            if nlayers < DEPTH:
                t_final = plan[nlayers][0] if st == 0 else tiles
            P.barrier()
            _final(P, nc, t_final, st, rmsnorm, BIG, yT)
            for eng in ("pe", "act", "dve"):
                for name_ in [k_ for k_ in P.dma_sems if k_.startswith("o_y")]:
                    h_, v_ = P.dma_sems[name_]
                    P.ops[eng].append(lambda e, h_=h_, v_=v_: e.wait_ge(h_, v_))

        P.finish("sp")
        P.run_block()
        nc._pe_log = P.pe_log
    return nc


def _final(P, nc, tiles, st, rmsnorm, BIG, yT):
    views = []
    for tt in range(3):
        o = tt * (2 * KC * 416)
        views.append(BIG[:, o:o + 2 * KC * 416].bitcast(F32).rearrange("p (c n) -> p c n", n=416))
    rmsnorm(tiles, "fing", 0, lambda kc, tt, c0, n: views[tt][:, kc, 0:n], lambda kc, tt: ("ys", tt, kc))
    for tt, (c0, n) in enumerate(tiles):
        if st == 0:
            lo = max(c0, 256)
            src = views[tt][:, :, lo - c0:n]
            dst = yT[:, :, lo - 256:c0 - 256 + n]
        else:
            src = views[tt][:, :, 0:n]
            dst = yT[:, :, 896 + c0:896 + c0 + n]
        for kc in range(KC):
            P.dma("sp", lambda e, src=src, dst=dst, kc=kc: e.dma_start(out=dst[:, kc, :], in_=src[:, kc, :]),
                  "o_y%d_%d" % (tt, kc), reads=[("ys", tt, kc)])


def _fm(v):
    v = np.asarray(v, dtype=np.float32)
    return np.ascontiguousarray(v.reshape(-1, 128).T)


_NC_CACHE = {}
_NLAYERS = DEPTH


def kernel(x_prompt, x_sample, state_conv, norm_mix_g, norm_ffn_g, norm_final_g,
           sgu_w_in, sgu_b_in, sgu_ln_g, sgu_ln_b, sgu_w_s, sgu_b_s, sgu_w_out, sgu_b_out,
           conv_w_pw1, conv_b_pw1, conv_w_dw, conv_b_dw, conv_ln_g, conv_ln_b, conv_w_pw2, conv_b_pw2,
           ffn_w_gate, ffn_w_up, ffn_w_down):
    f = lambda a: np.ascontiguousarray(np.asarray(a, dtype=np.float32))
    x_prompt = f(x_prompt); x_sample = f(x_sample); state_conv = f(state_conv)
    pp = np.zeros((128, PP_COLS), np.float32)

    def put(name, idx, vec):
        a = _fm(vec)
        o = PP_LAYOUT[name] + idx
        pp[:, o:o + a.shape[1]] = a
    for i in range(4):
        put("mixg", i * 8, norm_mix_g[i]); put("ffng", i * 8, norm_ffn_g[i])
    put("fing", 0, norm_final_g)
    for l in range(2):
        put("binu", l * 24, np.asarray(sgu_b_in)[l, :DSGU])
        put("lng", l * 24, sgu_ln_g[l]); put("lnb", l * 24, sgu_ln_b[l])
        put("bout", l * 8, sgu_b_out[l])
        put("bpw1", l * 16, conv_b_pw1[l])
        wd = np.asarray(conv_w_dw, np.float32)[l]
        wd_fm = wd.reshape(31, 8, 128).transpose(2, 1, 0).reshape(128, 248)
        o = PP_LAYOUT["wdw"] + l * 248
        pp[:, o:o + 248] = wd_fm
        put("bdw", l * 8, conv_b_dw[l]); put("clng", l * 8, conv_ln_g[l]); put("clnb", l * 8, conv_ln_b[l])
        put("bpw2", l * 8, conv_b_pw2[l])
    jj, ii = np.meshgrid(np.arange(128), np.arange(128), indexing="ij")
    mask = (jj <= ii).astype(np.float32)
    ident = np.eye(128, dtype=np.float32)
    wsT = np.ascontiguousarray(np.asarray(sgu_w_s, np.float32).transpose(0, 3, 1, 2))
    bs = f(np.asarray(sgu_b_s, np.float32).reshape(2, 512))
    binv = f(np.asarray(sgu_b_in, np.float32)[:, DSGU:])
    shared = {
        "pp": pp, "mask": mask, "ident": ident, "wsT": wsT, "bs": bs, "binv": binv,
        "lng_row": f(sgu_ln_g), "lnb_row": f(sgu_ln_b),
        "sgu_w_in": f(sgu_w_in), "sgu_w_out": f(sgu_w_out), "conv_w_pw1": f(conv_w_pw1),
        "conv_w_pw2": f(conv_w_pw2), "ffn_w_gate": f(ffn_w_gate), "ffn_w_up": f(ffn_w_up),
        "ffn_w_down": f(ffn_w_down),
    }
    in_maps = []
    for c in range(NCORES):
        b, half = c // 2, c % 2
        cols = np.zeros((T_ALL, D), np.float32)
        if half == 1:
            cols[0:256] = x_prompt[b, 1792:2048]
        cols[256:2304] = x_prompt[b, half * 2048:(half + 1) * 2048]
        cols[2304:2336] = x_sample[c]
        xT = np.ascontiguousarray(cols.reshape(T_ALL, 8, 128).transpose(2, 1, 0))
        pastT = np.ascontiguousarray(state_conv[:, c].reshape(2, 30, 8, 128).transpose(3, 0, 2, 1))
        m = dict(shared)
        m["xT"] = xT
        m["flag"] = np.full((128, 1), float(half), np.float32)
        m["pastT"] = pastT
        in_maps.append(m)
    if "nc" not in _NC_CACHE:
        _NC_CACHE["nc"] = build_program(nlayers=_NLAYERS)
    nc = _NC_CACHE["nc"]
    res = run_bass_kernel_spmd(nc, in_maps, core_ids=list(range(NCORES)))
    y_prompt = np.zeros((4, 4096, D), np.float32)
    y_sample = np.zeros((8, 32, D), np.float32)
    new_conv_prompt = np.zeros((2, 4, 30, D), np.float32)
    new_conv_sample = np.zeros((2, 8, 30, D), np.float32)
    new_v = np.zeros((2, 8, 32, DSGU), np.float32)
    for c in range(NCORES):
        r = res.results[c]
        b, half = c // 2, c % 2
        y = np.asarray(r["yT"]).transpose(2, 1, 0).reshape(2080, D)
        y_prompt[b, half * 2048:(half + 1) * 2048] = y[0:2048]
        y_sample[c] = y[2048:2080]
        cT = np.asarray(r["convT"])
        cv = cT.transpose(1, 2, 4, 3, 0).reshape(2, 2, 30, D)
        if half == 1:
            new_conv_prompt[:, b] = cv[:, 0]
        new_conv_sample[:, c] = cv[:, 1]
        new_v[:, c] = np.asarray(r["vout"])
    return (y_prompt, y_sample, new_conv_prompt, new_conv_sample, new_v)
```

```python
import sys
import numpy as np
from contextlib import ExitStack
import concourse.bass as bass
import concourse.mybir as mybir
from concourse.bass_utils import run_bass_kernel_spmd

F32 = mybir.dt.float32
BF16 = mybir.dt.bfloat16
AF = mybir.ActivationFunctionType
ALU = mybir.AluOpType

NCORES = 8
D = 1024
KC = 8
DEPTH = 4
DSGU = 3072
DFF = 2816
T_ALL = 256 + 2048 + 32
ST_COLS = [(0, 1152), (1152, 1184)]
RMS_EPS = 1e-6
LN_EPS = 1e-5
SLOT = 4096
NSLOT = 3
NDVE = 7

PP_LAYOUT = {}
_off = 0
for _name, _n in [("mixg", 32), ("ffng", 32), ("fing", 8), ("binu", 48), ("lng", 48), ("lnb", 48),
                  ("bout", 16), ("bpw1", 32), ("wdw", 496), ("bdw", 16), ("clng", 16), ("clnb", 16),
                  ("bpw2", 16)]:
    PP_LAYOUT[_name] = _off
    _off += _n
PP_COLS = _off


class Prog:
    ENGS = ("pe", "act", "dve", "pool", "sp")

    def __init__(self, nc, es):
        self.nc = nc
        self.es = es
        self.ops = {e: [] for e in self.ENGS}
        self.sem = {e: es.enter_context(nc.semaphore("s_" + e)) for e in self.ENGS}
        self.cnt = {e: 0 for e in self.ENGS}
        self.nins = {e: 0 for e in self.ENGS}
        self.last_ms_ins = {e: -10 for e in self.ENGS}
        self.waited = {e: {} for e in self.ENGS}
        self.res = {}
        self.dma_sems = {}
        self.phase = ""
        self.pe_log = []

    def _r(self, key):
        r = self.res.get(key)
        if r is None:
            r = [None, []]
            self.res[key] = r
        return r

    def _collect(self, reads, writes):
        deps = {}

        def add(d, true_dep):
            if d is None:
                return
            k, h, v = d
            if k not in deps or deps[k][1] < v:
                deps[k] = (h, v, true_dep or (k in deps and deps[k][2]))
            elif true_dep and deps[k][1] == v:
                deps[k] = (h, v, True)
        for r in reads:
            add(self._r(r)[0], True)
        for w in writes:
            rr = self._r(w)
            add(rr[0], True)
            for d in rr[1]:
                add(d, False)
        return deps

    def _emit_waits(self, eng, deps):
        for k, (h, v, true_dep) in deps.items():
            if k == eng:
                if eng in ("pe", "sp", "pool") or not true_dep:
                    continue
            if self.waited[eng].get(k, -1) >= v:
                continue
            self.waited[eng][k] = v
            self.ops[eng].append(lambda e, h=h, v=v: e.wait_ge(h, v))

    def _record(self, d, reads, writes):
        for r in reads:
            lst = self._r(r)[1]
            lst[:] = [x for x in lst if x[0] != d[0]]
            lst.append(d)
        for w in writes:
            rr = self._r(w)
            rr[0] = d
            rr[1] = []

    def op(self, eng, fns, reads=(), writes=()):
        if callable(fns):
            fns = [fns]
        deps = self._collect(reads, writes)
        self._emit_waits(eng, deps)
        if eng == "pe":
            self.pe_log.append((sys._getframe(1).f_lineno, len(fns)))
        self.cnt[eng] += 1
        v = self.cnt[eng]
        h = self.sem[eng]
        for f in fns[:-1]:
            self.ops[eng].append(f)
        last = fns[-1]
        self.ops[eng].append(lambda e, last=last, h=h: last(e).then_inc(h, 1))
        self.nins[eng] += len(fns)
        self.last_ms_ins[eng] = self.nins[eng] - 1
        d = (eng, h, v)
        self._record(d, reads, writes)
        return d

    def dma(self, q, fn, semname, reads=(), writes=(), skip_own=False):
        if semname not in self.dma_sems:
            self.dma_sems[semname] = [self.es.enter_context(self.nc.semaphore("d_" + semname)), 0]
        deps = self._collect(reads, writes)
        if skip_own:
            deps.pop("d_" + semname, None)
        self._emit_waits(q, deps)
        ent = self.dma_sems[semname]
        ent[1] += 16
        h, v = ent[0], ent[1]
        self.ops[q].append(lambda e, h=h: fn(e).then_inc(h, 16))
        self.nins[q] += 1
        d = ("d_" + semname, h, v)
        self._record(d, reads, writes)
        return d

    def barrier(self, engs=("pe", "act", "dve")):
        for e1 in engs:
            for e2 in engs:
                if e1 == e2 or self.cnt[e2] == 0:
                    continue
                v = self.cnt[e2]
                if self.waited[e1].get(e2, -1) >= v:
                    continue
                self.waited[e1][e2] = v
                self.ops[e1].append(lambda e, h=self.sem[e2], v=v: e.wait_ge(h, v))

    def finish(self, eng="sp"):
        for name, (h, v) in self.dma_sems.items():
            self.ops[eng].append(lambda e, h=h, v=v: e.wait_ge(h, v))
        for e2 in self.ENGS:
            if e2 != eng and self.cnt[e2] > 0:
                self.ops[eng].append(lambda e, h=self.sem[e2], v=self.cnt[e2]: e.wait_ge(h, v))

    def run_block(self):
        with self.nc.Block() as block:
            @block.tensor
            def _(e):
                for f in self.ops["pe"]:
                    f(e)

            @block.scalar
            def _(e):
                for f in self.ops["act"]:
                    f(e)

            @block.vector
            def _(e):
                for f in self.ops["dve"]:
                    f(e)

            @block.gpsimd
            def _(e):
                for f in self.ops["pool"]:
                    f(e)

            @block.sync
            def _(e):
                for f in self.ops["sp"]:
                    f(e)


def build_program(nlayers=DEPTH, nst=2):
    nc = bass.Bass("TRN2", target_bir_lowering=False)

    def din(name, shape):
        return nc.dram_tensor(name, list(shape), F32, kind="ExternalInput").ap()

    def dout(name, shape):
        return nc.dram_tensor(name, list(shape), F32, kind="ExternalOutput").ap()

    xT = din("xT", [128, KC, T_ALL])
    flag_d = din("flag", [128, 1])
    pastT = din("pastT", [128, 2, KC, 30])
    pp_d = din("pp", [128, PP_COLS])
    mask_d = din("mask", [128, 128])
    ident_d = din("ident", [128, 128])
    wsT_d = din("wsT", [2, 128, 4, 128])
    bs_d = din("bs", [2, 512])
    binv_d = din("binv", [2, DSGU])
    lngr_d = din("lng_row", [2, DSGU])
    lnbr_d = din("lnb_row", [2, DSGU])
    w_in = din("sgu_w_in", [2, D, 2 * DSGU])
    w_out = din("sgu_w_out", [2, DSGU, D])
    w_pw1 = din("conv_w_pw1", [2, D, 2 * D])
    w_pw2 = din("conv_w_pw2", [2, D, D])
    w_gate = din("ffn_w_gate", [DEPTH, D, DFF])
    w_up = din("ffn_w_up", [DEPTH, D, DFF])
    w_down = din("ffn_w_down", [DEPTH, DFF, D])

    yT = dout("yT", [128, KC, 2080])
    convT = dout("convT", [128, 2, 2, KC, 30])
    vout = dout("vout", [2, 32, DSGU])

    with ExitStack() as es:
        P = Prog(nc, es)

        def sb(name, shape, dt):
            return es.enter_context(nc.sbuf_tensor(name, list(shape), dt))

        xA = sb("xA", [128, KC, 1184], F32)
        hA = sb("hA", [128, KC, 1184], BF16)
        BIG = sb("BIG", [128, 30720], BF16)
        AUX = sb("AUX", [128, 13248], BF16)
        RING = sb("RING", [128, NSLOT, SLOT], BF16)
        SQ = sb("SQ", [128, 2, KC, 416], BF16)
        FT = sb("FT", [128, 6, 416], F32)
        WSF = sb("WSF", [128, 4, 128], F32)
        WSM = sb("WSM", [128, 4, 128], BF16)
        BSB = sb("BSB", [128, 4, 128], F32)
        MASK = sb("MASK", [128, 128], F32)
        IDENTB = sb("IDENTB", [128, 128], BF16)
        ONES = sb("ONES", [128, 128], BF16)
        PPS = sb("PPS", [128, PP_COLS], F32)
        PASTF = sb("PASTF", [128, 2, KC, 30], F32)
        TAIL = sb("TAIL", [128, 2, KC, 30], BF16)
        STATS = sb("STATS", [128, 10, 6, 6], F32)
        MV = sb("MV", [128, 10, 2], F32)
        SC = sb("SC", [128, 4, 10], F32)
        EPS = sb("EPS", [128, 2], F32)
        FLAG = sb("FLAG", [128, 1], F32)
        ps = es.enter_context(nc.psum_tensor("ps", [128, 8, 512], F32))
        VS = SQ[:].rearrange("p a k n -> p (a k n)")[:, 0:2 * DSGU].bitcast(F32)

        VN = BIG[:, 0:10 * DSGU].rearrange("p (c n) -> p c n", n=DSGU)
        A_ = BIG[:, 0:22 * 1184].rearrange("p (c n) -> p c n", n=1184)
        PD = BIG[:, 0:KC * 1246].rearrange("p (c n) -> p c n", n=1246)
        CC = BIG[:, 9984:9984 + 2 * KC * 1184].bitcast(F32).rearrange("p (c n) -> p c n", n=1184)
        YS = BIG[:, 0:2 * KC * 416].bitcast(F32).rearrange("p (c n) -> p c n", n=416)
        UG = AUX[:, 0:6 * 1184].rearrange("p (c n) -> p c n", n=1184)
        CB = AUX[:, 7104:7104 + 2 * 24 * 128].bitcast(F32).rearrange("p (c n) -> p c n", n=128)
        _c0 = 7104
        GB = FT[:, 0:3, :].rearrange("p a n -> p (a n)")[:, 0:1024].rearrange("p (c n) -> p c n", n=512)
        BRF = AUX[:, _c0 + 2048:_c0 + 3072].bitcast(F32)
        BRL32 = AUX[:, _c0 + 3072:_c0 + 4096].bitcast(F32)
        BRH2 = AUX[:, _c0 + 4096:_c0 + 5120].rearrange("p (c n) -> p c n", n=512)
        BRLO = AUX[:, _c0 + 5120:_c0 + 5632]
        DIAG = AUX[:, 0:2 * 31 * 128].rearrange("p (b k n) -> p b k n", b=2, k=31)
        CST = AUX[:, 7936:7936 + 2 * KC * 62].bitcast(F32).rearrange("p (c n) -> p c n", n=62)
        DIAG7 = AUX[:, 8928:8928 + NDVE * 128].rearrange("p (k n) -> p k n", n=128)

        def pcol(name, idx):
            o = PP_LAYOUT[name] + idx
            return PPS[:, o:o + 1]

        bank_ctr = [0]

        def newbank():
            b = bank_ctr[0] % 8
            bank_ctr[0] += 1
            return b

        slot_ctr = [0]

        def ring_load(parts):
            s = slot_ctr[0] % NSLOT
            slot_ctr[0] += 1
            views = []
            off = 0
            keys = []
            for i, src in enumerate(parts):
                K, n = src.shape
                kc = K // 128
                view = RING[:, s, off:off + kc * n].rearrange("p (k n) -> p k n", n=n)
                wk = [("ring", s, 0), ("ring", s, 1)] if i == 0 else [("ring", s, i)]
                P.dma("pool", lambda e, view=view, src=src: e.dma_start(
                    out=view, in_=src.rearrange("(k p) n -> p k n", p=128)),
                    "ring%d" % s, writes=wk, skip_own=True)
                views.append(view)
                keys.append(("ring", s, i))
                off += kc * n
            assert off <= SLOT
            return views, keys

        P.dma("sp", lambda e: e.dma_start(out=PPS[:], in_=pp_d), "c0", writes=["pp"])
        P.dma("sp", lambda e: e.dma_start(out=MASK[:], in_=mask_d), "c1", writes=["mask"])
        P.dma("sp", lambda e: e.dma_start(out=WSF[:, 0, :], in_=ident_d), "c2", writes=["wsf"])
        P.dma("sp", lambda e: e.dma_start(out=FLAG[:], in_=flag_d), "c3", writes=["flag"])
        P.dma("sp", lambda e: e.dma_start(out=PASTF[:], in_=pastT), "c4", writes=["pastf"])
        P.op("dve", lambda e: e.memset(ONES[:], 1.0), writes=["ones"])
        P.op("dve", lambda e: e.memset(EPS[:, 0:1], RMS_EPS), writes=["eps0"])
        P.op("dve", lambda e: e.memset(EPS[:, 1:2], LN_EPS), writes=["eps1"])
        P.op("dve", lambda e: e.tensor_copy(out=IDENTB[:], in_=WSF[:, 0, :]), reads=["wsf"], writes=["identb"])

        def x_keys(tt):
            return [("x", kc, tt) for kc in range(KC)]

        def h_keys(tt):
            return [("h", kc, tt) for kc in range(KC)]

        def rmsnorm(tiles, gname, gidx, out_fn, out_keys_fn):
            banks = {}

            def stage_a(tt):
                c0, n = tiles[tt]
                par = tt % 2
                for kc in range(KC):
                    P.op("act", lambda e, kc=kc, c0=c0, n=n, par=par: e.activation(
                        out=SQ[:, par, kc, 0:n], in_=xA[:, kc, c0:c0 + n], func=AF.Square),
                        reads=[("x", kc, tt)], writes=[("sq", par, kc)])
                b = newbank()
                banks[tt] = b
                P.op("pe", [lambda e, kc=kc, n=n, par=par, b=b: e.matmul(
                    ps[:, b, 0:n], lhsT=ONES[:, :], rhs=SQ[:, par, kc, 0:n], start=(kc == 0), stop=(kc == KC - 1))
                    for kc in range(KC)],
                    reads=["ones"] + [("sq", par, kc) for kc in range(KC)], writes=[("ps", b)])

            def stage_b(tt):
                c0, n = tiles[tt]
                par = tt % 2
                b = banks[tt]
                P.op("act", lambda e, n=n, par=par, b=b: e.activation(
                    out=FT[:, par, 0:n], in_=ps[:, b, 0:n], func=AF.Sqrt, scale=1.0 / D, bias=EPS[:, 0:1]),
                    reads=[("ps", b), "eps0"], writes=[("ft", par)])
                P.op("dve", lambda e, n=n, par=par: e.reciprocal(out=FT[:, 2 + par, 0:n], in_=FT[:, par, 0:n]),
                     reads=[("ft", par)], writes=[("ft", 2 + par)])
                for kc in range(KC):
                    P.op("dve", lambda e, kc=kc, c0=c0, n=n, par=par, tt=tt: e.scalar_tensor_tensor(
                        out=out_fn(kc, tt, c0, n), in0=xA[:, kc, c0:c0 + n], scalar=pcol(gname, gidx * KC + kc),
                        in1=FT[:, 2 + par, 0:n], op0=ALU.mult, op1=ALU.mult),
                        reads=[("x", kc, tt), ("ft", 2 + par), "pp"], writes=[out_keys_fn(kc, tt)])
            nt = len(tiles)
            stage_a(0)
            for tt in range(nt):
                if tt + 1 < nt:
                    stage_a(tt + 1)
                stage_b(tt)

        def norm_to_h(tiles, gname, gidx):
            rmsnorm(tiles, gname, gidx, lambda kc, tt, c0, n: hA[:, kc, c0:c0 + n], lambda kc, tt: ("h", kc, tt))

        aux_snap = {"pe": 0, "act": 0, "dve": 0}

        def snap_aux():
            for e2 in aux_snap:
                aux_snap[e2] = P.cnt[e2]

        def ffn(layer, tiles):
            P.barrier()
            norm_to_h(tiles, "ffng", layer)
            for mb in range(11):
                (wg, wu), keys = ring_load([w_gate[layer, :, mb * 256:(mb + 1) * 256],
                                            w_up[layer, :, mb * 256:(mb + 1) * 256]])
                for tt, (c0, n) in enumerate(tiles):
                    for j in range(2):
                        mc = mb * 2 + j
                        par = (tt * 2 + j) % 2
                        bg = newbank()
                        bu = newbank()
                        for (w_, b_) in ((wg, bg), (wu, bu)):
                            P.op("pe", [lambda e, kc=kc, w_=w_, b_=b_, c0=c0, n=n, j=j: e.matmul(
                                ps[:, b_, 0:n], lhsT=w_[:, kc, j * 128:(j + 1) * 128], rhs=hA[:, kc, c0:c0 + n],
                                start=(kc == 0), stop=(kc == KC - 1)) for kc in range(KC)],
                                reads=keys + h_keys(tt), writes=[("ps", b_)])
                        P.op("act", lambda e, bg=bg, n=n, par=par: e.activation(
                            out=FT[:, 4 + par, 0:n], in_=ps[:, bg, 0:n], func=AF.Silu),
                            reads=[("ps", bg)], writes=[("ft", 4 + par)])
                        P.op("dve", lambda e, bu=bu, n=n, par=par, mc=mc, c0=c0: e.tensor_tensor(
                            out=A_[:, mc, c0:c0 + n], in0=ps[:, bu, 0:n], in1=FT[:, 4 + par, 0:n], op=ALU.mult),
                            reads=[("ps", bu), ("ft", 4 + par)], writes=[("a", mc, tt)])
            for m in range(KC):
                (sW,), kW = ring_load([w_down[layer, :, m * 128:(m + 1) * 128]])
                for tt, (c0, n) in enumerate(tiles):
                    b = newbank()
                    P.op("pe", [lambda e, kc=kc, b=b, c0=c0, n=n, sW=sW: e.matmul(
                        ps[:, b, 0:n], lhsT=sW[:, kc, :],
                        rhs=A_[:, kc, c0:c0 + n], start=(kc == 0), stop=(kc == 21)) for kc in range(22)],
                        reads=kW + [("a", kc, tt) for kc in range(22)], writes=[("ps", b)])
                    P.op("dve", lambda e, b=b, m=m, c0=c0, n=n: e.tensor_tensor(
                        out=xA[:, m, c0:c0 + n], in0=ps[:, b, 0:n], in1=xA[:, m, c0:c0 + n], op=ALU.add),
                        reads=[("ps", b), ("x", m, tt)], writes=[("x", m, tt)])

        def sgu(l, layer, tiles, has_sample):
            P.barrier()
            for e2 in ("pe", "act", "dve"):
                if aux_snap[e2] > 0:
                    P.ops["sp"].append(lambda e, h_=P.sem[e2], v_=aux_snap[e2]: e.wait_ge(h_, v_))
            if "o_c" in P.dma_sems:
                h_, v_ = P.dma_sems["o_c"]
                P.ops["sp"].append(lambda e, h_=h_, v_=v_: e.wait_ge(h_, v_))
            P.dma("sp", lambda e: e.dma_start(out=WSF[:], in_=wsT_d[l]), "c2", writes=["wsf"])
            P.dma("sp", lambda e: e.dma_start(out=BSB[:].rearrange("p h i -> p (h i)"),
                                              in_=bs_d[l].partition_broadcast(128)), "c5", writes=["bsb"])

            def prep_bias(nb):
                bp = nb % 2
                P.dma("sp", lambda e, nb=nb: e.dma_start(out=BRF[0:1, :], in_=binv_d[l:l + 1, nb * 512:(nb + 1) * 512]),
                      "c6", writes=["brf"])
                P.op("dve", lambda e, bp=bp: e.tensor_copy(out=BRH2[0:1, bp, :], in_=BRF[0:1, :]),
                     reads=["brf"], writes=[("brh", bp)])
                P.op("dve", lambda e, bp=bp: e.tensor_tensor(out=BRL32[0:1, :], in0=BRF[0:1, :], in1=BRH2[0:1, bp, :],
                                                             op=ALU.subtract),
                     reads=["brf", ("brh", bp)], writes=["brl32"])
                P.op("dve", lambda e: e.tensor_copy(out=BRLO[0:1, :], in_=BRL32[0:1, :]),
                     reads=["brl32"], writes=["brlo"])
                P.dma("sp", lambda e, bp=bp: e.dma_start(out=BRH2[1:2, bp, :], in_=BRLO[0:1, :]), "c7",
                      reads=["brlo"], writes=[("brh1", bp)])
            prep_bias(0)
            for hd in range(4):
                P.op("dve", lambda e, hd=hd: e.tensor_tensor(out=WSM[:, hd, :], in0=WSF[:, hd, :], in1=MASK[:],
                                                             op=ALU.mult),
                     reads=["wsf", "mask"], writes=[("wsm", hd)])
            norm_to_h(tiles, "mixg", layer)
            chunk_list = []
            chunk_tile = {}
            for tt_, (c0_, n_) in enumerate(tiles):
                for cc_ in range(c0_, c0_ + (n_ // 128) * 128, 128):
                    chunk_list.append((cc_ // 128, cc_, 128))
                    chunk_tile[cc_ // 128] = tt_
            cmin, cmax1 = chunk_list[0][0], chunk_list[-1][0] + 1
            if has_sample:
                chunk_list.append((9, 1152, 32))
                chunk_tile[9] = 2
            for nb in range(6):
                (wv,), kv = ring_load([w_in[l, :, DSGU + nb * 512:DSGU + (nb + 1) * 512]])
                bp = nb % 2
                if nb + 1 < 6:
                    prep_bias(nb + 1)
                for (ci, cc0, rows) in chunk_list:
                    b = newbank()
                    hk = h_keys(chunk_tile[ci])
                    P.op("pe", [lambda e, kc=kc, b=b, cc0=cc0, rows=rows, wv=wv: e.matmul(
                        ps[0:rows, b, :], lhsT=hA[:, kc, cc0:cc0 + rows], rhs=wv[:, kc, :],
                        start=(kc == 0), stop=False) for kc in range(KC)]
                        + [lambda e, b=b, rows=rows, bp=bp: e.matmul(
                            ps[0:rows, b, :], lhsT=ONES[0:2, 0:rows], rhs=BRH2[0:2, bp, :], start=False, stop=True)],
                        reads=kv + hk + ["ones", ("brh", bp), ("brh1", bp)], writes=[("ps", b)])
                    if ci < 9:
                        dst = VN[:, ci, nb * 512:(nb + 1) * 512]
                        dkey = ("vn", ci, nb)
                    else:
                        dst = VS[0:32, nb * 512:(nb + 1) * 512]
                        dkey = ("vs", nb)
                    P.op("act", lambda e, b=b, rows=rows, dst=dst: e.activation(
                        out=dst, in_=ps[0:rows, b, :], func=AF.Gelu_apprx_tanh),
                        reads=[("ps", b)], writes=[dkey])
                    P.op("dve", lambda e, ci=ci, nb=nb, rows=rows, dst=dst: e.bn_stats(
                        out=STATS[0:rows, ci, nb, :], in_=dst), reads=[dkey], writes=[("stats", ci, nb)])
            for (ci, cc0, rows) in chunk_list:
                P.op("dve", lambda e, ci=ci, rows=rows: e.bn_aggr(
                    out=MV[0:rows, ci, :], in_=STATS[0:rows, ci, :, :].rearrange("p a b -> p (a b)")),
                    reads=[("stats", ci, nb) for nb in range(6)], writes=[("mv", ci)])
            groups = [(cmin, cmax1, 128)] + ([(9, 10, 32)] if has_sample else [])
            for (g0, g1, rows) in groups:
                gk = ("scg", g0)
                P.op("dve", lambda e, g0=g0, g1=g1, rows=rows: e.tensor_scalar(
                    out=SC[0:rows, 0, g0:g1], in0=MV[0:rows, g0:g1, 1], scalar1=LN_EPS, scalar2=None, op0=ALU.add),
                    reads=[("mv", ci) for ci in range(g0, g1)], writes=[gk + (0,)])
                P.op("act", lambda e, g0=g0, g1=g1, rows=rows: e.activation(
                    out=SC[0:rows, 1, g0:g1], in_=SC[0:rows, 0, g0:g1], func=AF.Sqrt),
                    reads=[gk + (0,)], writes=[gk + (1,)])
                P.op("dve", lambda e, g0=g0, g1=g1, rows=rows: e.reciprocal(
                    out=SC[0:rows, 2, g0:g1], in_=SC[0:rows, 1, g0:g1]), reads=[gk + (1,)], writes=[gk + (2,)])
                P.op("dve", lambda e, g0=g0, g1=g1, rows=rows: e.scalar_tensor_tensor(
                    out=SC[0:rows, 3, g0:g1], in0=MV[0:rows, g0:g1, 0], scalar=-1.0, in1=SC[0:rows, 2, g0:g1],
                    op0=ALU.mult, op1=ALU.mult),
                    reads=[("mv", ci) for ci in range(g0, g1)] + [gk + (2,)], writes=[gk + (3,)])
            for ci in range(cmin, cmax1):
                P.op("dve", lambda e, ci=ci: e.tensor_scalar(
                    out=VN[:, ci, :], in0=VN[:, ci, :], scalar1=SC[:, 2, ci:ci + 1], scalar2=SC[:, 3, ci:ci + 1],
                    op0=ALU.mult, op1=ALU.add),
                    reads=[("vn", ci, nb) for nb in range(6)] + [("scg", cmin, 2), ("scg", cmin, 3)],
                    writes=[("vnn", ci)])
            if has_sample:
                P.op("dve", lambda e: e.tensor_scalar(
                    out=VN[0:32, 9, :], in0=VS[0:32, :], scalar1=SC[0:32, 2, 9:10], scalar2=SC[0:32, 3, 9:10],
                    op0=ALU.mult, op1=ALU.add),
                    reads=[("vs", nb) for nb in range(6)] + [("scg", 9, 2), ("scg", 9, 3)], writes=[("vnn", 9)])
                P.op("dve", lambda e: e.tensor_scalar(
                    out=VS[0:32, :], in0=VS[0:32, :], scalar1=SC[0:32, 2, 9:10], scalar2=SC[0:32, 3, 9:10],
                    op0=ALU.mult, op1=ALU.add),
                    reads=[("vs", nb) for nb in range(6)] + [("scg", 9, 2), ("scg", 9, 3), ("vnn", 9)],
                    writes=[("vs", nb) for nb in range(6)])
            def sample_affine(nb):
                P.dma("sp", lambda e, nb=nb: e.dma_start(
                    out=GB[0:32, 0, :], in_=lngr_d[l, nb * 512:(nb + 1) * 512].partition_broadcast(32)),
                    "c8", writes=["gb0", ("ft", 0), ("ft", 1)])
                P.dma("sp", lambda e, nb=nb: e.dma_start(
                    out=GB[0:32, 1, :], in_=lnbr_d[l, nb * 512:(nb + 1) * 512].partition_broadcast(32)),
                    "c9", writes=["gb1", ("ft", 1), ("ft", 2)])
                P.op("dve", lambda e, nb=nb: e.tensor_tensor(
                    out=VS[0:32, nb * 512:(nb + 1) * 512], in0=VS[0:32, nb * 512:(nb + 1) * 512],
                    in1=GB[0:32, 0, :], op=ALU.mult), reads=[("vs", nb), "gb0", ("ft", 0), ("ft", 1)],
                    writes=[("vs", nb)])
                P.op("dve", lambda e, nb=nb: e.tensor_tensor(
                    out=VS[0:32, nb * 512:(nb + 1) * 512], in0=VS[0:32, nb * 512:(nb + 1) * 512],
                    in1=GB[0:32, 1, :], op=ALU.add), reads=[("vs", nb), "gb1", ("ft", 1), ("ft", 2)],
                    writes=[("vs", nb)])
            if P.cnt["pe"] > 0:
                P.ops["dve"].append(lambda e, h_=P.sem["pe"], v_=P.cnt["pe"]: e.wait_ge(h_, v_))
            brs = newbank()
            P.op("pe", [lambda e, hd=hd, brs=brs: e.matmul(
                ps[:, brs, hd * 128:(hd + 1) * 128], lhsT=ONES[:, :], rhs=WSM[:, hd, :], start=True, stop=True)
                for hd in range(4)],
                reads=["ones"] + [("wsm", hd) for hd in range(4)], writes=[("ps", brs)])
            RSB = FT[:, 4:6, :].rearrange("p a n -> p (a n)")
            P.op("act", lambda e, brs=brs: e.activation(out=RSB[:, 0:512], in_=ps[:, brs, :], func=AF.Copy),
                 reads=[("ps", brs)], writes=[("ft", 4), ("ft", 5)])
            for dc in range(24):
                hd = dc // 6
                P.op("dve", lambda e, dc=dc, hd=hd: e.scalar_tensor_tensor(
                    out=CB[:, dc, :], in0=RSB[:, hd * 128:(hd + 1) * 128], scalar=pcol("lnb", l * 24 + dc),
                    in1=BSB[:, hd, :], op0=ALU.mult, op1=ALU.add),
                    reads=[("ft", 4), ("ft", 5), "bsb", "pp"], writes=[("cb", dc)])
            for hd in range(4):
                wus = []
                kus = []
                for s in range(2):
                    (w_,), k_ = ring_load([w_in[l, :, hd * 768 + s * 384: hd * 768 + (s + 1) * 384]])
                    wus.append(w_)
                    kus.append(k_)
                def u_groups(mc):
                    dc = hd * 6 + mc
                    s, off = mc // 3, (mc % 3) * 128
                    for tt, (c0, n) in enumerate(tiles):
                        b = newbank()
                        P.op("pe", [lambda e, kc=kc, b=b, c0=c0, n=n, w_=wus[s], off=off: e.matmul(
                            ps[:, b, 0:n], lhsT=w_[:, kc, off:off + 128], rhs=hA[:, kc, c0:c0 + n],
                            start=(kc == 0), stop=(kc == KC - 1)) for kc in range(KC)],
                            reads=kus[s] + h_keys(tt), writes=[("ps", b)])
                        P.op("act", lambda e, b=b, mc=mc, c0=c0, n=n, dc=dc: e.activation(
                            out=UG[:, mc, c0:c0 + n], in_=ps[:, b, 0:n], func=AF.Gelu_apprx_tanh,
                            bias=pcol("binu", l * 24 + dc)),
                            reads=[("ps", b), "pp"], writes=[("ug", mc, tt)])

                def mix_one(mc, tt):
                    c0, n = tiles[tt]
                    nch = n // 128
                    nm = nch * 128
                    par = tt % 2
                    if True:
                        dc = hd * 6 + mc
                        b = newbank()
                        mms = []
                        rk = [("wsm", hd)]
                        for jj in range(nch):
                            ci = c0 // 128 + jj
                            mms.append(lambda e, b=b, jj=jj, ci=ci, dc=dc, hd=hd: e.matmul(
                                ps[:, b, jj * 128:(jj + 1) * 128], lhsT=VN[:, ci, dc * 128:(dc + 1) * 128],
                                rhs=WSM[:, hd, :], start=True, stop=True))
                            rk.append(("vnn", ci))
                        if n > nm:
                            mms.append(lambda e, b=b, dc=dc, hd=hd: e.matmul(
                                ps[:, b, nm:nm + 32], lhsT=VN[0:32, 9, dc * 128:(dc + 1) * 128],
                                rhs=WSM[0:32, hd, 0:32], start=True, stop=True))
                            rk.append(("vnn", 9))
                        P.op("pe", mms, reads=rk, writes=[("ps", b)])
                        P.op("dve", lambda e, b=b, par=par, dc=dc: e.scalar_tensor_tensor(
                            out=FT[:, 4 + par, 0:nm].rearrange("p (a i) -> p a i", i=128),
                            in0=ps[:, b, 0:nm].rearrange("p (a i) -> p a i", i=128),
                            scalar=pcol("lng", l * 24 + dc), in1=CB[:, dc:dc + 1, :].to_broadcast([128, nch, 128]),
                            op0=ALU.mult, op1=ALU.add),
                            reads=[("ps", b), ("cb", dc), "pp"], writes=[("ft", 4 + par)])
                        if n > nm:
                            P.op("dve", lambda e, b=b, par=par, dc=dc: e.scalar_tensor_tensor(
                                out=FT[:, 4 + par, nm:nm + 32], in0=ps[:, b, nm:nm + 32],
                                scalar=pcol("lng", l * 24 + dc), in1=CB[:, dc, 0:32],
                                op0=ALU.mult, op1=ALU.add),
                                reads=[("ps", b), ("cb", dc), "pp"], writes=[("ft", 4 + par, "s")])
                        P.op("dve", lambda e, par=par, mc=mc, c0=c0, n=n: e.tensor_tensor(
                            out=UG[:, mc, c0:c0 + n], in0=FT[:, 4 + par, 0:n], in1=UG[:, mc, c0:c0 + n], op=ALU.mult),
                            reads=[("ft", 4 + par), ("ft", 4 + par, "s"), ("ug", mc, tt)], writes=[("ug", mc, tt)])

                for mc in range(6):
                    u_groups(mc)
                    for tt in range(len(tiles)):
                        mix_one(mc, tt)
                    if has_sample and mc == 2:
                        sample_affine(hd)
                    if has_sample and mc == 4 and hd < 2:
                        sample_affine(4 + hd)
                wos = []
                kos = []
                for s in range(2):
                    (w_,), k_ = ring_load([w_out[l, hd * 768:(hd + 1) * 768, s * 512:(s + 1) * 512]])
                    wos.append(w_)
                    kos.append(k_)
                for m in range(KC):
                    s, off = m // 4, (m % 4) * 128
                    for tt, (c0, n) in enumerate(tiles):
                        b = newbank()
                        P.op("pe", [lambda e, kc=kc, b=b, c0=c0, n=n, w_=wos[s], off=off: e.matmul(
                            ps[:, b, 0:n], lhsT=w_[:, kc, off:off + 128], rhs=UG[:, kc, c0:c0 + n],
                            start=(kc == 0), stop=(kc == 5)) for kc in range(6)],
                            reads=kos[s] + [("ug", kc, tt) for kc in range(6)], writes=[("ps", b)])
                        sc = pcol("bout", l * 8 + m) if hd == 0 else 0.0
                        P.op("dve", lambda e, b=b, m=m, c0=c0, n=n, sc=sc: e.scalar_tensor_tensor(
                            out=xA[:, m, c0:c0 + n], in0=ps[:, b, 0:n], scalar=sc, in1=xA[:, m, c0:c0 + n],
                            op0=ALU.add, op1=ALU.add),
                            reads=[("ps", b), ("x", m, tt), "pp"], writes=[("x", m, tt)])
            if has_sample:
                P.dma("sp", lambda e: e.dma_start(out=vout[l], in_=VS[0:32, :]), "o_v",
                      reads=[("vs", nb) for nb in range(6)])
                h_, v_ = P.dma_sems["o_v"]
                for eng in ("act", "dve"):
                    P.ops[eng].append(lambda e, h_=h_, v_=v_: e.wait_ge(h_, v_))

        def conv(l, layer, tiles_c1, tiles, st):
            nmain = 1152
            def build_diag(kc, ks):
                dp = kc % 2
                for k in ks:
                    P.op("act", lambda e, kc=kc, k=k, dp=dp: e.activation(
                        out=DIAG[:, dp, k, :], in_=IDENTB[:, :], func=AF.Copy,
                        scale=pcol("wdw", l * 248 + kc * 31 + k)),
                        reads=["identb", "pp"], writes=[("diag", dp, k)])
            for e2 in ("pe", "dve"):
                if aux_snap[e2] > 0:
                    P.ops["act"].append(lambda e, h_=P.sem[e2], v_=aux_snap[e2]: e.wait_ge(h_, v_))
            if "o_c" in P.dma_sems:
                h_, v_ = P.dma_sems["o_c"]
                P.ops["act"].append(lambda e, h_=h_, v_=v_: e.wait_ge(h_, v_))
            build_diag(0, range(NDVE, 31))
            for k in range(NDVE):
                P.op("act", lambda e, k=k: e.activation(
                    out=DIAG7[:, k, :], in_=IDENTB[:, :], func=AF.Copy,
                    scale=pcol("wdw", l * 248 + (KC - 1) * 31 + k)),
                    reads=["identb", "pp"], writes=[("diag7", k)])
            P.barrier()
            norm_to_h(tiles_c1, "mixg", layer)
            if st == 0:
                P.op("dve", lambda e: e.memset(PD[:, :, 0:30], 0.0), writes=["pd_past"])
            else:
                P.op("act", lambda e: e.activation(out=PD[:, :, 0:30], in_=TAIL[:, l, :, :], func=AF.Copy),
                     reads=[("tail", l)], writes=["pd_past"])
                P.op("act", lambda e: e.activation(out=PD[:, :, 1182:1212], in_=PASTF[:, l, :, :], func=AF.Copy),
                     reads=["pastf"], writes=["pd_spast"])
            for m2 in range(4):
                (wa, wb), keys = ring_load([w_pw1[l, :, m2 * 256:(m2 + 1) * 256],
                                            w_pw1[l, :, D + m2 * 256:D + (m2 + 1) * 256]])
                for tt, (c0, n) in enumerate(tiles_c1):
                    for j in range(2):
                        m = m2 * 2 + j
                        par = (tt * 2 + j) % 2
                        nm = (n // 128) * 128
                        ba = newbank()
                        bb = newbank()
                        for (w_, b_) in ((wa, ba), (wb, bb)):
                            P.op("pe", [lambda e, kc=kc, w_=w_, b_=b_, c0=c0, n=n, j=j: e.matmul(
                                ps[:, b_, 0:n], lhsT=w_[:, kc, j * 128:(j + 1) * 128], rhs=hA[:, kc, c0:c0 + n],
                                start=(kc == 0), stop=(kc == KC - 1)) for kc in range(KC)],
                                reads=keys + h_keys(tt), writes=[("ps", b_)])
                        P.op("act", lambda e, bb=bb, n=n, par=par, m=m: e.activation(
                            out=FT[:, 4 + par, 0:n], in_=ps[:, bb, 0:n], func=AF.Sigmoid,
                            bias=pcol("bpw1", l * 16 + 8 + m)),
                            reads=[("ps", bb), "pp"], writes=[("ft", 4 + par)])
                        P.op("dve", lambda e, ba=ba, par=par, m=m, c0=c0: e.scalar_tensor_tensor(
                            out=PD[:, m, 30 + c0:30 + c0 + nm], in0=ps[:, ba, 0:nm],
                            scalar=pcol("bpw1", l * 16 + m), in1=FT[:, 4 + par, 0:nm], op0=ALU.add, op1=ALU.mult),
                            reads=[("ps", ba), ("ft", 4 + par), "pp"], writes=[("pd", m, tt)])
                        if st == 0 and tt == 0:
                            P.op("dve", lambda e, m=m, c0=c0: e.tensor_scalar(
                                out=PD[:, m, 30 + c0:30 + 256], in0=PD[:, m, 30 + c0:30 + 256], scalar1=FLAG[:, 0:1],
                                scalar2=None, op0=ALU.mult),
                                reads=[("pd", m, tt), "flag"], writes=[("pd", m, tt)])
                        if n > nm:
                            P.op("dve", lambda e, ba=ba, par=par, m=m: e.scalar_tensor_tensor(
                                out=PD[:, m, 1212:1244], in0=ps[:, ba, nm:nm + 32],
                                scalar=pcol("bpw1", l * 16 + m), in1=FT[:, 4 + par, nm:nm + 32],
                                op0=ALU.add, op1=ALU.mult),
                                reads=[("ps", ba), ("ft", 4 + par), "pp"], writes=[("pds", m)])
                            P.op("dve", lambda e, ba=ba, par=par, m=m: e.scalar_tensor_tensor(
                                out=CST[:, m, :], in0=ps[:, ba, nm - 30:nm + 32],
                                scalar=pcol("bpw1", l * 16 + m), in1=FT[:, 4 + par, nm - 30:nm + 32],
                                op0=ALU.add, op1=ALU.mult),
                                reads=[("ps", ba), ("ft", 4 + par), "pp"], writes=[("cst", m)])
            if st == 0:
                P.op("act", lambda e: e.activation(out=TAIL[:, l, :, :], in_=PD[:, :, 1152:1182], func=AF.Copy),
                     reads=[("pd", m, 2) for m in range(KC)], writes=[("tail", l)])
            else:
                P.dma("sp", lambda e: e.dma_start(out=convT[:, l, 0, :, :], in_=CST[:, :, 0:30]), "o_c",
                      reads=[("cst", m) for m in range(KC)])
                P.dma("sp", lambda e: e.dma_start(out=convT[:, l, 1, :, :], in_=CST[:, :, 32:62]), "o_c",
                      reads=[("cst", m) for m in range(KC)])
            ln_banks = {}

            def ln_cs(tt):
                c0, n = tiles[tt]
                for kc in range(KC):
                    P.op("dve", lambda e, kc=kc, c0=c0, n=n: e.tensor_copy(
                        out=SQ[:, 0, kc, 0:n], in_=CC[:, kc, c0:c0 + n]),
                        reads=[("cc", kc, tt)], writes=[("sq", 0, kc)])
                    P.op("act", lambda e, kc=kc, c0=c0, n=n: e.activation(
                        out=SQ[:, 1, kc, 0:n], in_=CC[:, kc, c0:c0 + n], func=AF.Square),
                        reads=[("cc", kc, tt)], writes=[("sq", 1, kc)])

            def ln_stats(tt):
                c0, n = tiles[tt]
                b1 = newbank()
                b2 = newbank()
                ln_banks[tt] = (b1, b2)
                for (q, b_) in ((0, b1), (1, b2)):
                    P.op("pe", [lambda e, kc=kc, n=n, q=q, b_=b_: e.matmul(
                        ps[:, b_, 0:n], lhsT=ONES[:, :], rhs=SQ[:, q, kc, 0:n], start=(kc == 0), stop=(kc == KC - 1))
                        for kc in range(KC)],
                        reads=["ones"] + [("sq", q, kc) for kc in range(KC)], writes=[("ps", b_)])

            def ln_b1(tt):
                c0, n = tiles[tt]
                b1, b2 = ln_banks[tt]
                P.op("act", lambda e, b1=b1, n=n: e.activation(
                    out=FT[:, 4, 0:n], in_=ps[:, b1, 0:n], func=AF.Copy, scale=1.0 / D),
                    reads=[("ps", b1)], writes=[("ft", 4)])
                P.op("act", lambda e, b1=b1, n=n: e.activation(
                    out=FT[:, 5, 0:n], in_=ps[:, b1, 0:n], func=AF.Square, scale=1.0 / D),
                    reads=[("ps", b1)], writes=[("ft", 5)])
                P.op("dve", lambda e, b2=b2, n=n: e.scalar_tensor_tensor(
                    out=FT[:, 0, 0:n], in0=ps[:, b2, 0:n], scalar=1.0 / D, in1=FT[:, 5, 0:n],
                    op0=ALU.mult, op1=ALU.subtract),
                    reads=[("ps", b2), ("ft", 5)], writes=[("ft", 0)])
                P.op("act", lambda e, n=n: e.activation(
                    out=FT[:, 1, 0:n], in_=FT[:, 0, 0:n], func=AF.Sqrt, bias=EPS[:, 1:2]),
                    reads=[("ft", 0), "eps1"], writes=[("ft", 1)])
                P.op("dve", lambda e, n=n: e.reciprocal(out=FT[:, 2, 0:n], in_=FT[:, 1, 0:n]),
                     reads=[("ft", 1)], writes=[("ft", 2)])

            def ln_b2(tt):
                c0, n = tiles[tt]
                for kc in range(KC):
                    P.op("dve", lambda e, kc=kc, c0=c0, n=n: e.tensor_tensor(
                        out=CC[:, kc, c0:c0 + n], in0=CC[:, kc, c0:c0 + n], in1=FT[:, 4, 0:n], op=ALU.subtract),
                        reads=[("cc", kc, tt), ("ft", 4), ("sq", 0, kc), ("sq", 1, kc)], writes=[("cc", kc, tt)])
                for kc in range(KC):
                    P.op("dve", lambda e, kc=kc, c0=c0, n=n: e.tensor_tensor(
                        out=CC[:, kc, c0:c0 + n], in0=CC[:, kc, c0:c0 + n], in1=FT[:, 2, 0:n], op=ALU.mult),
                        reads=[("cc", kc, tt), ("ft", 2)], writes=[("cc", kc, tt)])
                    P.op("act", lambda e, kc=kc, c0=c0, n=n: e.activation(
                        out=hA[:, kc, c0:c0 + n], in_=CC[:, kc, c0:c0 + n], func=AF.Silu,
                        scale=pcol("clng", l * 8 + kc), bias=pcol("clnb", l * 8 + kc)),
                        reads=[("cc", kc, tt), "pp"], writes=[("h", kc, tt)])

            def ndve_of(kc):
                return NDVE if kc < KC - 1 else 0

            def kparts_of(kc):
                ks = list(range(NDVE, 31))
                a = len(ks) // 3
                return [ks[0:a], ks[a:2 * a], ks[2 * a:]]
            for kc in range(KC):
                dp = kc % 2
                nd = ndve_of(kc)
                for tt, (c0, n) in enumerate(tiles):
                    nm = (n // 128) * 128
                    if kc + 1 < KC:
                        build_diag(kc + 1, kparts_of(kc + 1)[tt])
                    b = newbank()
                    mms = [lambda e, k=k, b=b, kc=kc, dp=dp, c0=c0, nm=nm, nd=nd: e.matmul(
                        ps[:, b, 0:nm], lhsT=(DIAG7[:, k, :] if k < NDVE else DIAG[:, dp, k, :]),
                        rhs=PD[:, kc, c0 + k:c0 + k + nm],
                        start=(k == nd), stop=(k == 30)) for k in range(nd, 31)]
                    rk = [(("diag7", k) if k < NDVE else ("diag", dp, k)) for k in range(nd, 31)] \
                        + [("pd", kc, tt), "pd_past"]
                    if tt > 0:
                        rk.append(("pd", kc, tt - 1))
                    if n > nm:
                        mms += [lambda e, k=k, b=b, kc=kc, dp=dp, nm=nm, nd=nd: e.matmul(
                            ps[:, b, nm:nm + 32], lhsT=(DIAG7[:, k, :] if k < NDVE else DIAG[:, dp, k, :]),
                            rhs=PD[:, kc, 1182 + k:1182 + k + 32],
                            start=(k == nd), stop=(k == 30)) for k in range(nd, 31)]
                        rk += [("pds", kc), "pd_spast"]
                    P.op("pe", mms, reads=rk, writes=[("ps", b)])
                    P.op("act", lambda e, b=b, kc=kc, c0=c0, n=n: e.activation(
                        out=CC[:, kc, c0:c0 + n], in_=ps[:, b, 0:n], func=AF.Identity, bias=pcol("bdw", l * 8 + kc)),
                        reads=[("ps", b), "pp"], writes=[("cc", kc, tt)])
                    if kc == KC - 1:
                        if tt > 0:
                            ln_stats(tt - 1)
                            ln_b1(tt - 1)
                        ln_cs(tt)
                        if tt > 0:
                            ln_b2(tt - 1)
                for k in range(nd):
                    for tt, (c0, n) in enumerate(tiles):
                        nm = (n // 128) * 128
                        rk = [("cc", kc, tt), ("pd", kc, tt), "pd_past", "pp"]
                        if tt > 0:
                            rk.append(("pd", kc, tt - 1))
                        P.op("dve", lambda e, k=k, kc=kc, c0=c0, nm=nm: e.scalar_tensor_tensor(
                            out=CC[:, kc, c0:c0 + nm], in0=PD[:, kc, c0 + k:c0 + k + nm],
                            scalar=pcol("wdw", l * 248 + kc * 31 + k), in1=CC[:, kc, c0:c0 + nm],
                            op0=ALU.mult, op1=ALU.add), reads=rk, writes=[("cc", kc, tt)])
                        if n > nm:
                            P.op("dve", lambda e, k=k, kc=kc, c0=c0, nm=nm: e.scalar_tensor_tensor(
                                out=CC[:, kc, c0 + nm:c0 + nm + 32], in0=PD[:, kc, 1182 + k:1182 + k + 32],
                                scalar=pcol("wdw", l * 248 + kc * 31 + k), in1=CC[:, kc, c0 + nm:c0 + nm + 32],
                                op0=ALU.mult, op1=ALU.add),
                                reads=[("pds", kc), "pd_spast", "pp"],
                                writes=[("cc", kc, tt)])
            pw2w = []
            pw2k = []
            for s in range(2):
                (w_,), k_ = ring_load([w_pw2[l, :, s * 512:(s + 1) * 512]])
                pw2w.append(w_)
                pw2k.append(k_)

            def pw2_tile(tt):
                c0, n = tiles[tt]
                for m in range(KC):
                    s, j = m // 4, m % 4
                    b = newbank()
                    P.op("pe", [lambda e, kc=kc, b=b, w_=pw2w[s], j=j: e.matmul(
                        ps[:, b, 0:n], lhsT=w_[:, kc, j * 128:(j + 1) * 128], rhs=hA[:, kc, c0:c0 + n],
                        start=(kc == 0), stop=(kc == KC - 1)) for kc in range(KC)],
                        reads=pw2k[s] + h_keys(tt), writes=[("ps", b)])
                    P.op("dve", lambda e, b=b, m=m: e.scalar_tensor_tensor(
                        out=xA[:, m, c0:c0 + n], in0=ps[:, b, 0:n], scalar=pcol("bpw2", l * 8 + m),
                        in1=xA[:, m, c0:c0 + n], op0=ALU.add, op1=ALU.add),
                        reads=[("ps", b), ("x", m, tt), "pp"], writes=[("x", m, tt)])

            nt = len(tiles)
            ln_stats(nt - 1)
            ln_b1(nt - 1)
            pw2_tile(0)
            ln_b2(nt - 1)
            for tt in range(1, nt):
                pw2_tile(tt)

        for st in range(nst):
            g0, T = ST_COLS[st]
            tiles = [(0, 384), (384, 384), (768, T - 768)]
            P.barrier()
            for tt, (c0, n) in enumerate(tiles):
                for kc in range(KC):
                    P.dma("sp", lambda e, c0=c0, n=n, g0=g0, kc=kc: e.dma_start(
                        out=xA[:, kc, c0:c0 + n], in_=xT[:, kc, g0 + c0:g0 + c0 + n]),
                        "xin%d_%d" % (tt, kc), writes=[("x", kc, tt)])
            if st == 0:
                t_noh0 = [(128, 256), (384, 384), (768, 384)]
                t_main = [(256, 128), (384, 384), (768, 384)]
                plan = [(tiles, tiles, tiles), (tiles, t_noh0, t_noh0), (t_noh0, t_noh0, t_noh0),
                        (t_noh0, t_main, t_main)]
                t_final = t_main
            else:
                plan = [(tiles, tiles, tiles)] * 4
                t_final = tiles
            for layer in range(nlayers):
                l = layer // 2
                t_a, t_b, t_f = plan[layer]
                if layer % 2 == 0:
                    sgu(l, layer, t_a, st == 1)
                else:
                    conv(l, layer, t_a, t_b, st)
                snap_aux()
                ffn(layer, t_f)
            if nlayers < DEPTH:
                t_final = plan[nlayers][0] if st == 0 else tiles
            P.barrier()
            _final(P, nc, t_final, st, rmsnorm, BIG, yT)
            for eng in ("pe", "act", "dve"):
                for name_ in [k_ for k_ in P.dma_sems if k_.startswith("o_y")]:
                    h_, v_ = P.dma_sems[name_]
                    P.ops[eng].append(lambda e, h_=h_, v_=v_: e.wait_ge(h_, v_))

        P.finish("sp")
        P.run_block()
        nc._pe_log = P.pe_log
    return nc


def _final(P, nc, tiles, st, rmsnorm, BIG, yT):
    views = []
    for tt in range(3):
        o = tt * (2 * KC * 416)
        views.append(BIG[:, o:o + 2 * KC * 416].bitcast(F32).rearrange("p (c n) -> p c n", n=416))
    rmsnorm(tiles, "fing", 0, lambda kc, tt, c0, n: views[tt][:, kc, 0:n], lambda kc, tt: ("ys", tt, kc))
    for tt, (c0, n) in enumerate(tiles):
        if st == 0:
            lo = max(c0, 256)
            src = views[tt][:, :, lo - c0:n]
            dst = yT[:, :, lo - 256:c0 - 256 + n]
        else:
            src = views[tt][:, :, 0:n]
            dst = yT[:, :, 896 + c0:896 + c0 + n]
        for kc in range(KC):
            P.dma("sp", lambda e, src=src, dst=dst, kc=kc: e.dma_start(out=dst[:, kc, :], in_=src[:, kc, :]),
                  "o_y%d_%d" % (tt, kc), reads=[("ys", tt, kc)])


def _fm(v):
    v = np.asarray(v, dtype=np.float32)
    return np.ascontiguousarray(v.reshape(-1, 128).T)


_NC_CACHE = {}
_NLAYERS = DEPTH


def kernel(x_prompt, x_sample, state_conv, norm_mix_g, norm_ffn_g, norm_final_g,
           sgu_w_in, sgu_b_in, sgu_ln_g, sgu_ln_b, sgu_w_s, sgu_b_s, sgu_w_out, sgu_b_out,
           conv_w_pw1, conv_b_pw1, conv_w_dw, conv_b_dw, conv_ln_g, conv_ln_b, conv_w_pw2, conv_b_pw2,
           ffn_w_gate, ffn_w_up, ffn_w_down):
    f = lambda a: np.ascontiguousarray(np.asarray(a, dtype=np.float32))
    x_prompt = f(x_prompt); x_sample = f(x_sample); state_conv = f(state_conv)
    pp = np.zeros((128, PP_COLS), np.float32)

    def put(name, idx, vec):
        a = _fm(vec)
        o = PP_LAYOUT[name] + idx
        pp[:, o:o + a.shape[1]] = a
    for i in range(4):
        put("mixg", i * 8, norm_mix_g[i]); put("ffng", i * 8, norm_ffn_g[i])
    put("fing", 0, norm_final_g)
    for l in range(2):
        put("binu", l * 24, np.asarray(sgu_b_in)[l, :DSGU])
        put("lng", l * 24, sgu_ln_g[l]); put("lnb", l * 24, sgu_ln_b[l])
        put("bout", l * 8, sgu_b_out[l])
        put("bpw1", l * 16, conv_b_pw1[l])
        wd = np.asarray(conv_w_dw, np.float32)[l]
        wd_fm = wd.reshape(31, 8, 128).transpose(2, 1, 0).reshape(128, 248)
        o = PP_LAYOUT["wdw"] + l * 248
        pp[:, o:o + 248] = wd_fm
        put("bdw", l * 8, conv_b_dw[l]); put("clng", l * 8, conv_ln_g[l]); put("clnb", l * 8, conv_ln_b[l])
        put("bpw2", l * 8, conv_b_pw2[l])
    jj, ii = np.meshgrid(np.arange(128), np.arange(128), indexing="ij")
    mask = (jj <= ii).astype(np.float32)
    ident = np.eye(128, dtype=np.float32)
    wsT = np.ascontiguousarray(np.asarray(sgu_w_s, np.float32).transpose(0, 3, 1, 2))
    bs = f(np.asarray(sgu_b_s, np.float32).reshape(2, 512))
    binv = f(np.asarray(sgu_b_in, np.float32)[:, DSGU:])
    shared = {
        "pp": pp, "mask": mask, "ident": ident, "wsT": wsT, "bs": bs, "binv": binv,
        "lng_row": f(sgu_ln_g), "lnb_row": f(sgu_ln_b),
        "sgu_w_in": f(sgu_w_in), "sgu_w_out": f(sgu_w_out), "conv_w_pw1": f(conv_w_pw1),
        "conv_w_pw2": f(conv_w_pw2), "ffn_w_gate": f(ffn_w_gate), "ffn_w_up": f(ffn_w_up),
        "ffn_w_down": f(ffn_w_down),
    }
    in_maps = []
    for c in range(NCORES):
        b, half = c // 2, c % 2
        cols = np.zeros((T_ALL, D), np.float32)
        if half == 1:
            cols[0:256] = x_prompt[b, 1792:2048]
        cols[256:2304] = x_prompt[b, half * 2048:(half + 1) * 2048]
        cols[2304:2336] = x_sample[c]
        xT = np.ascontiguousarray(cols.reshape(T_ALL, 8, 128).transpose(2, 1, 0))
        pastT = np.ascontiguousarray(state_conv[:, c].reshape(2, 30, 8, 128).transpose(3, 0, 2, 1))
        m = dict(shared)
        m["xT"] = xT
        m["flag"] = np.full((128, 1), float(half), np.float32)
        m["pastT"] = pastT
        in_maps.append(m)
    if "nc" not in _NC_CACHE:
        _NC_CACHE["nc"] = build_program(nlayers=_NLAYERS)
    nc = _NC_CACHE["nc"]
    res = run_bass_kernel_spmd(nc, in_maps, core_ids=list(range(NCORES)))
    y_prompt = np.zeros((4, 4096, D), np.float32)
    y_sample = np.zeros((8, 32, D), np.float32)
    new_conv_prompt = np.zeros((2, 4, 30, D), np.float32)
    new_conv_sample = np.zeros((2, 8, 30, D), np.float32)
    new_v = np.zeros((2, 8, 32, DSGU), np.float32)
    for c in range(NCORES):
        r = res.results[c]
        b, half = c // 2, c % 2
        y = np.asarray(r["yT"]).transpose(2, 1, 0).reshape(2080, D)
        y_prompt[b, half * 2048:(half + 1) * 2048] = y[0:2048]
        y_sample[c] = y[2048:2080]
        cT = np.asarray(r["convT"])
        cv = cT.transpose(1, 2, 4, 3, 0).reshape(2, 2, 30, D)
        if half == 1:
            new_conv_prompt[:, b] = cv[:, 0]
        new_conv_sample[:, c] = cv[:, 1]
        new_v[:, c] = np.asarray(r["vout"])
    return (y_prompt, y_sample, new_conv_prompt, new_conv_sample, new_v)
```
